# Optimizing a Trainium2 kernel written in Bass

```python
import jax, jax.numpy as jnp
from jax import lax
import numpy as np

D_MODEL = 1024
BATCH = 8
SEQ = 4096
DEPTH = 1

GRID_W = 64
CTX_LEN = 256
D_MIX = D_MODEL
D_ATTN = D_MIX // 2
D_POOL = D_MIX - D_ATTN
MLA_HEADS = 4
QK_NOPE_DIM = 128
QK_ROPE_DIM = 64
QK_HEAD_DIM = QK_NOPE_DIM + QK_ROPE_DIM
V_HEAD_DIM = D_ATTN // MLA_HEADS
Q_LORA_RANK = 256
KV_LORA_RANK = 128
ROPE_AXIS_DIM = QK_ROPE_DIM // 2
ROPE_BASE = 10000.0
POOL_WINDOWS = (2, 4, 8, 16)
POOL_GROUPS = len(POOL_WINDOWS)
POOL_GROUP_DIM = D_POOL // POOL_GROUPS
Q_BLOCK = 128
NORM_EPS = 1e-6

OFF_CQ = 0
OFF_CKV = OFF_CQ + Q_LORA_RANK
OFF_KR = OFF_CKV + KV_LORA_RANK
OFF_GA = OFF_KR + QK_ROPE_DIM
OFF_PIN = OFF_GA + D_ATTN
OFF_GP = OFF_PIN + D_POOL
D_IN_PROJ = OFF_GP + D_POOL

kernel_name = "hymba_mla_pool_adaln_prefix"


def _rms(x, g):
    xf = x.astype(jnp.float32)
    y = xf * lax.rsqrt(jnp.mean(xf * xf, axis=-1, keepdims=True) + NORM_EPS)
    return (y * g.astype(jnp.float32)).astype(x.dtype)


def _rotate_half(x):
    x1, x2 = jnp.split(x, 2, axis=-1)
    return jnp.concatenate([-x2, x1], axis=-1)


def _axial_rope_tables(L):
    rows = L // GRID_W
    row = jnp.repeat(jnp.arange(rows, dtype=jnp.float32), GRID_W)
    col = jnp.tile(jnp.arange(GRID_W, dtype=jnp.float32), rows)
    n_freq = ROPE_AXIS_DIM // 2
    inv = ROPE_BASE ** (-jnp.arange(n_freq, dtype=jnp.float32) / n_freq)
    ang_r = row[:, None] * inv
    ang_c = col[:, None] * inv
    ang = jnp.concatenate([ang_r, ang_r, ang_c, ang_c], axis=-1)
    return jnp.cos(ang), jnp.sin(ang)


def _apply_axial_rope(x, cos, sin):
    xr, xc = jnp.split(x, 2, axis=-1)
    rot = jnp.concatenate([_rotate_half(xr), _rotate_half(xc)], axis=-1)
    c = cos[:, None, :]
    s = sin[:, None, :]
    return (x.astype(jnp.float32) * c + rot.astype(jnp.float32) * s).astype(x.dtype)


def _mla_qkv(u, q_lora_g, w_uq, kv_lora_g, w_ukv, q_norm_g, k_norm_g, rope):
    B, L = u.shape[0], u.shape[1]
    cq = u[..., OFF_CQ:OFF_CKV]
    ckv = u[..., OFF_CKV:OFF_KR]
    k_rope = u[..., OFF_KR:OFF_GA]
    q = (_rms(cq, q_lora_g) @ w_uq).reshape(B, L, MLA_HEADS, QK_HEAD_DIM)
    kv = (_rms(ckv, kv_lora_g) @ w_ukv).reshape(B, L, MLA_HEADS, QK_NOPE_DIM + V_HEAD_DIM)
    k_nope, v = kv[..., :QK_NOPE_DIM], kv[..., QK_NOPE_DIM:]
    k = jnp.concatenate(
        [k_nope, jnp.broadcast_to(k_rope[:, :, None, :], (B, L, MLA_HEADS, QK_ROPE_DIM))], axis=-1)
    q = _rms(q, q_norm_g)
    k = _rms(k, k_norm_g)
    if rope is not None:
        cos, sin = rope
        q = jnp.concatenate([q[..., :QK_NOPE_DIM], _apply_axial_rope(q[..., QK_NOPE_DIM:], cos, sin)], axis=-1)
        k = jnp.concatenate([k[..., :QK_NOPE_DIM], _apply_axial_rope(k[..., QK_NOPE_DIM:], cos, sin)], axis=-1)
    tr = lambda t: jnp.transpose(t, (0, 2, 1, 3))
    return tr(q), tr(k), tr(v)


def _attend(q, k, v):
    B, H, Lq, dk = q.shape
    nb = Lq // Q_BLOCK
    scale = QK_HEAD_DIM ** -0.5
    qb = jnp.transpose(q.reshape(B, H, nb, Q_BLOCK, dk), (2, 0, 1, 3, 4))

    def one(qblk):
        s = jnp.einsum('bhqd,bhkd->bhqk', qblk, k).astype(jnp.float32) * scale
        p = jax.nn.softmax(s, axis=-1)
        return jnp.einsum('bhqk,bhkd->bhqd', p.astype(v.dtype), v)

    o = lax.map(one, qb)
    o = jnp.transpose(o, (1, 3, 0, 2, 4)).reshape(B, Lq, H * v.shape[-1])
    return o


def _multiscale_pool(u, w_pool, pool_scale):
    B, L, _ = u.shape
    ug = u.reshape(B, L, POOL_GROUPS, POOL_GROUP_DIM)
    cs = jnp.concatenate(
        [jnp.zeros((B, 1, POOL_GROUPS, POOL_GROUP_DIM), jnp.float32),
         jnp.cumsum(ug.astype(jnp.float32), axis=1)], axis=1)
    t = jnp.arange(L, dtype=jnp.int32)[:, None]
    w = jnp.array(POOL_WINDOWS, dtype=jnp.int32)[None, :]
    lo = jnp.clip(t - w // 2, 0, L)
    hi = jnp.clip(t - w // 2 + w, 0, L)
    g = jnp.arange(POOL_GROUPS, dtype=jnp.int32)[None, :]
    win_sum = cs[:, hi, g, :] - cs[:, lo, g, :]
    cnt = (hi - lo).astype(jnp.float32)[None, :, :, None]
    pooled = (win_sum / cnt - ug.astype(jnp.float32)).astype(u.dtype)
    y = jnp.einsum('blgc,gcd->blgd', pooled, w_pool).reshape(B, L, D_POOL)
    return y * pool_scale


def _branches(u, mla_out, w_pool, pool_scale):
    gate_a = u[..., OFF_GA:OFF_PIN]
    pool_in = u[..., OFF_PIN:OFF_GP]
    gate_p = u[..., OFF_GP:D_IN_PROJ]
    br_a = jax.nn.silu(gate_a) * mla_out
    br_p = jax.nn.silu(gate_p) * _multiscale_pool(pool_in, w_pool, pool_scale)
    return jnp.concatenate([br_a, br_p], axis=-1)


def setup_inputs(seed: int = 0) -> dict:
    key = jax.random.key(seed)
    ks = jax.random.split(key, 20)
    f32 = jnp.float32
    nrm = lambda k, shape, s: jax.random.normal(k, shape, f32) * s
    return {
        "x": nrm(ks[0], (BATCH, SEQ, D_MODEL), 1.0),
        "c": nrm(ks[1], (BATCH, D_MODEL), 1.0),
        "ctx": nrm(ks[2], (BATCH, CTX_LEN, D_MODEL), 1.0),
        "c_ctx": nrm(ks[3], (D_MODEL,), 1.0),
        "w_mod": nrm(ks[4], (DEPTH, D_MODEL, 3 * D_MODEL), 0.5 * D_MODEL ** -0.5),
        "b_mod": nrm(ks[5], (DEPTH, 3 * D_MODEL), 0.02),
        "norm_g": 1.0 + nrm(ks[6], (DEPTH, D_MODEL), 0.1),
        "w_in": nrm(ks[7], (DEPTH, D_MODEL, D_IN_PROJ), D_MODEL ** -0.5),
        "q_lora_g": 1.0 + nrm(ks[8], (DEPTH, Q_LORA_RANK), 0.1),
        "w_uq": nrm(ks[9], (DEPTH, Q_LORA_RANK, MLA_HEADS * QK_HEAD_DIM), Q_LORA_RANK ** -0.5),
        "kv_lora_g": 1.0 + nrm(ks[10], (DEPTH, KV_LORA_RANK), 0.1),
        "w_ukv": nrm(ks[11], (DEPTH, KV_LORA_RANK, MLA_HEADS * (QK_NOPE_DIM + V_HEAD_DIM)), KV_LORA_RANK ** -0.5),
        "q_norm_g": 1.0 + nrm(ks[12], (DEPTH, QK_HEAD_DIM), 0.1),
        "k_norm_g": 1.0 + nrm(ks[13], (DEPTH, QK_HEAD_DIM), 0.1),
        "w_pool": nrm(ks[14], (DEPTH, POOL_GROUPS, POOL_GROUP_DIM, POOL_GROUP_DIM), POOL_GROUP_DIM ** -0.5),
        "pool_scale": 1.0 + nrm(ks[15], (DEPTH, D_POOL), 0.1),
        "w_out": nrm(ks[16], (DEPTH, D_MIX, D_MODEL), D_MIX ** -0.5),
    }


def reference(x, c, ctx, c_ctx, w_mod, b_mod, norm_g, w_in, q_lora_g, w_uq, kv_lora_g, w_ukv,
              q_norm_g, k_norm_g, w_pool, pool_scale, w_out):
    L = x.shape[1]
    rope = _axial_rope_tables(L)
    for l in range(DEPTH):
        mod = jax.nn.silu(c) @ w_mod[l] + b_mod[l]
        shift, scale, gate = jnp.split(mod, 3, axis=-1)
        mod_c = jax.nn.silu(c_ctx) @ w_mod[l] + b_mod[l]
        shift_c, scale_c, gate_c = jnp.split(mod_c, 3, axis=-1)

        h = _rms(x, norm_g[l]) * (1.0 + scale[:, None, :]) + shift[:, None, :]
        hc = _rms(ctx, norm_g[l]) * (1.0 + scale_c) + shift_c
        u = h @ w_in[l]
        uc = hc @ w_in[l]

        q, k, v = _mla_qkv(u, q_lora_g[l], w_uq[l], kv_lora_g[l], w_ukv[l],
                           q_norm_g[l], k_norm_g[l], rope)
        qc, kc, vc = _mla_qkv(uc, q_lora_g[l], w_uq[l], kv_lora_g[l], w_ukv[l],
                              q_norm_g[l], k_norm_g[l], None)
        k_all = jnp.concatenate([kc, k], axis=2)
        v_all = jnp.concatenate([vc, v], axis=2)
        attn = _attend(q, k_all, v_all)
        y = _branches(u, attn, w_pool[l], pool_scale[l]) @ w_out[l]
        x_new = x + gate[:, None, :] * y

        if l < DEPTH - 1:
            attn_c = _attend(qc, kc, vc)
            yc = _branches(uc, attn_c, w_pool[l], pool_scale[l]) @ w_out[l]
            ctx = ctx + gate_c * yc
        x = x_new
    return x
```

```python
import os
import numpy as np
from contextlib import ExitStack
import concourse.bass as bass
import concourse.mybir as mybir
from concourse.bass_utils import run_bass_kernel_spmd

F32 = mybir.dt.float32
BF16 = mybir.dt.bfloat16
AF = mybir.ActivationFunctionType
ALU = mybir.AluOpType

D = 1024
L = 4096
CTX = 256
NTOK = L + CTX
NKT = NTOK // 128
EPS = 1e-6
NV = 45
SAME_ENGINE_SYNC = {"act": True, "dve": True, "pool": True, "pe": False, "sp": True}
_DSZ = {F32: 4, BF16: 2}


class _Op:
    __slots__ = ("eng", "fn", "deps", "dma", "sig")

    def __init__(self, eng, fn, deps, dma):
        self.eng, self.fn, self.deps, self.dma, self.sig = eng, fn, deps, dma, None


class Sched:
    def __init__(self, name):
        self.name = name
        self.ops = []
        self.acc = {}
        self.final = {}
        self.closed = False

    @staticmethod
    def box(ap):
        t = ap.tensor
        pstep = 1
        for s in list(t.shape)[1:]:
            pstep *= int(s)
        off = int(ap.offset)
        dims = list(ap.ap)
        p0 = off // pstep
        f0 = off % pstep
        ext = 0
        for st, cn in dims[1:]:
            ext += abs(int(st)) * (int(cn) - 1)
        p1 = p0 + int(dims[0][1])
        if type(t).__name__.startswith("PSum"):
            return (t.name, (p0 // 32) * 32, ((p1 + 31) // 32) * 32, 0, pstep)
        return (t.name, p0, p1, f0, f0 + ext + 1)

    def add(self, eng, fn, reads=(), writes=(), dma=None):
        if self.closed:
            return -1
        idx = len(self.ops)
        mx = os.environ.get("KMAXOPS")
        if mx is not None and self.name == "a1":
            if idx >= int(mx):
                self.closed = True
                return -1
            import traceback
            self.dbg = getattr(self, "dbg", [])
            self.dbg.append((eng, [f.lineno for f in traceback.extract_stack(limit=5)[:-1]]))
        deps = set()
        sk = dma if dma is not None else eng
        for ap in reads:
            nm, p0, p1, f0, f1 = self.box(ap)
            lst = self.acc.setdefault(nm, [])
            found = False
            is_psum = type(ap.tensor).__name__.startswith("PSum")
            for e in lst:
                if e[0] < p1 and p0 < e[1] and e[2] < f1 and f0 < e[3]:
                    if e[5] or (is_psum and e[6] != sk):
                        deps.add(e[4])
                    elif (not found) and e[6] == sk and e[0] == p0 and e[1] == p1 and e[2] == f0 and e[3] == f1:
                        e[4] = idx
                        found = True
            if not found:
                lst.append([p0, p1, f0, f1, idx, False, sk])
        for ap in writes:
            nm, p0, p1, f0, f1 = self.box(ap)
            lst = self.acc.setdefault(nm, [])
            keep = []
            for e in lst:
                if e[0] < p1 and p0 < e[1] and e[2] < f1 and f0 < e[3]:
                    deps.add(e[4])
                    if e[0] >= p0 and e[1] <= p1 and e[2] >= f0 and e[3] <= f1:
                        continue
                keep.append(e)
            keep.append([p0, p1, f0, f1, idx, True, sk])
            self.acc[nm] = keep
        deps.discard(idx)
        self.ops.append(_Op(eng, fn, sorted(deps), dma))
        return idx

    @staticmethod
    def _needs(p, o):
        if p.dma is not None or o.dma is not None:
            return True
        if p.eng != o.eng:
            return True
        return SAME_ENGINE_SYNC.get(o.eng, True)

    def finalize(self):
        n = len(self.ops)
        needed = [False] * n
        for o in self.ops:
            for d in o.deps:
                if self._needs(self.ops[d], o):
                    needed[d] = True
        cnt = {}
        last = {}
        for i, o in enumerate(self.ops):
            last[o.eng] = i
        for i, o in enumerate(self.ops):
            if o.dma is not None:
                cnt[o.dma] = cnt.get(o.dma, 0) + 16
                o.sig = (o.dma, cnt[o.dma])
            elif needed[i] or last[o.eng] == i:
                cnt[o.eng] = cnt.get(o.eng, 0) + 1
                o.sig = (o.eng, cnt[o.eng])
        self.final = dict(cnt)
        return list(cnt.keys())

    def emit_engine(self, engname, e, sems, prologue=()):
        waited = {}
        for sem, val in prologue:
            e.wait_ge(sem, val)
        for o in self.ops:
            if o.eng != engname:
                continue
            w = {}
            for d in o.deps:
                p = self.ops[d]
                if p.sig is None or not self._needs(p, o):
                    continue
                k, v = p.sig
                if waited.get(k, 0) >= v:
                    continue
                if w.get(k, 0) < v:
                    w[k] = v
            for k, v in w.items():
                e.wait_ge(sems[k], v)
                waited[k] = v
            ins = o.fn(e)
            if o.sig is not None:
                ins.then_inc(sems[o.sig[0]], 16 if o.dma is not None else 1)


def _run_block(nc, es, sched, prev, final_waits=False):
    keys = sched.finalize()
    sems = {}
    for k in keys:
        nm = "s_%s_%s" % (sched.name, "_".join(str(x) for x in (k if isinstance(k, tuple) else (k,))))
        sems[k] = es.enter_context(nc.semaphore(nm))
    with nc.Block() as block:
        def mk(engname, with_final):
            def body(e):
                sched.emit_engine(engname, e, sems, prologue=prev)
                if with_final:
                    for k, v in sched.final.items():
                        e.wait_ge(sems[k], v)
            return body

        block.tensor(mk("pe", False))
        block.scalar(mk("act", False))
        block.vector(mk("dve", False))
        block.gpsimd(mk("pool", False))
        block.sync(mk("sp", final_waits))
    return [(sems[k], v) for k, v in sched.final.items()]


def build_nc(stop=None):
    nc = bass.Bass("TRN2", target_bir_lowering=False)
    dram = lambda n, shp: nc.dram_tensor(n, shp, F32, kind="ExternalInput").ap()
    x_d = dram("x", [L, D])
    ctx_d = dram("ctx", [CTX, D])
    cT_d = dram("cT", [128, 16])
    vecs_d = dram("vecs", [128, NV])
    bgate_d = dram("bgate", [1, D])
    wmod_d = dram("w_mod", [D, 3 * D])
    wina_d = dram("w_in_a", [D, 640])
    winb_d = dram("w_in_b", [D, 1536])
    wuq_d = dram("w_uq_x", [256, 1024])
    wukv_d = dram("w_ukv_x", [128, 1024])
    wpool_d = dram("w_pool", [4, 128, 128])
    wout_d = dram("w_out", [D, D])
    tabs_d = dram("tabs", [128, 2, NTOK])
    ident_d = dram("ident", [128, 128])
    invc_d = dram("invc", [128, 64])
    out_d = nc.dram_tensor("out", [L, D], F32, kind="ExternalOutput").ap()

    with ExitStack() as es:
        sb = lambda n, shp, dt: es.enter_context(nc.sbuf_tensor(n, shp, dt))
        QN = sb("QN", [128, 4 * L], BF16)
        gate_bc = sb("gate_bc", [128, D], F32)
        r_all = sb("r_all", [128, 36], F32)
        vecs = sb("vecs_sb", [128, NV], F32)
        Gm = sb("Gm", [128, 16], F32)
        SH = sb("SH", [128, 16], F32)
        ident = sb("ident_sb", [128, 128], BF16)
        ones_b = sb("ones_b", [128, 128], BF16)
        sel0 = sb("sel0", [128, 128], BF16)
        sel1 = sb("sel1", [128, 128], BF16)
        ones_f = sb("ones_f", [128, 128], F32)
        eps_t = sb("eps_t", [128, 1], F32)
        sels = [sel0, sel1]

        prev = []
        with ExitStack() as esA:
            sbA = lambda n, shp, dt: esA.enter_context(nc.sbuf_tensor(n, shp, dt))
            QR = sbA("QR", [128, 2 * L], BF16)
            KN = sbA("KN", [128, 4 * NTOK], BF16)
            KR = sbA("KR", [128, 2 * NTOK], BF16)
            V = sbA("V", [128, NKT * 512], BF16)
            w_in_a = sbA("w_in_a_sb", [128, 8 * 640], BF16)
            w_uq = sbA("w_uq_sb", [128, 2 * 1024], BF16)
            w_ukv = sbA("w_ukv_sb", [128, 1024], BF16)
            xin = sbA("xin", [128, 2 * D], F32)
            xs = sbA("xs", [128, 2 * D], BF16)
            hT = sbA("hT", [128, 8 * 512], BF16)
            tabs = sbA("tabs_sb", [128, 2 * 512], F32)
            fpool = sbA("fpool", [128, 8 * 512], F32)
            bpool = sbA("bpool", [128, 8 * 512], BF16)
            cT = sbA("cT_sb", [128, 16], F32)
            e16 = sbA("e16", [128, 16], F32)
            scb = sbA("scb", [128, 16], BF16)
            modsb = sbA("modsb", [128, 32], F32)
            sstmp = sbA("sstmp", [128, 4], F32)
            lntmp = sbA("lntmp", [128, 4], F32)

            def FP(i, T=512):
                return fpool[:, i * 512:i * 512 + T]

            def BP(i, T=512):
                return bpool[:, i * 512:i * 512 + T]

            bgate = fpool[0:1, 4 * 512:6 * 512]
            gate_row = fpool[0:1, 6 * 512:8 * 512]

            with ExitStack() as es1:
                tp = [es1.enter_context(nc.psum_tensor("tp%d" % i, [128, 1024], BF16)) for i in range(4)]
                mm = [es1.enter_context(nc.psum_tensor("mm%d" % i, [128, 512], F32)) for i in range(4)]
                S = Sched("a1")

                def dma(q, out, in_, key, reads=(), writes=()):
                    S.add(q, lambda e: e.dma_start(out=out, in_=in_), reads=reads, writes=writes, dma=key)

                def mmul(out, lhsT, rhs, start=True, stop=True):
                    S.add("pe", lambda e: e.matmul(out, lhsT, rhs, start=start, stop=stop),
                          reads=[lhsT, rhs] + ([] if start else [out]), writes=[out])

                def act(out, in_, func, scale=1.0, bias=None, accum=None):
                    rd = [in_]
                    kw = {}
                    if not isinstance(scale, (int, float)):
                        rd.append(scale)
                    if bias is not None:
                        kw["bias"] = bias
                        if not isinstance(bias, (int, float)):
                            rd.append(bias)
                    wr = [out]
                    if accum is not None:
                        kw["accum_out"] = accum
                        wr.append(accum)
                    S.add("act", lambda e: e.activation(out, in_, func, scale=scale, **kw), reads=rd, writes=wr)

                def tt(eng, out, in0, in1, op):
                    S.add(eng, lambda e: e.tensor_tensor(out, in0, in1, op), reads=[in0, in1], writes=[out])

                def ts(eng, out, in0, s1, s2, op0, op1=None):
                    rd = [in0] + [s for s in (s1, s2) if s is not None and not isinstance(s, (int, float))]
                    if op1 is None and eng == "pool":
                        S.add(eng, lambda e: e.tensor_scalar(out, in0, s1, 0.0, op0, ALU.add), reads=rd, writes=[out])
                    elif op1 is None:
                        S.add(eng, lambda e: e.tensor_scalar(out, in0, s1, None, op0), reads=rd, writes=[out])
                    else:
                        S.add(eng, lambda e: e.tensor_scalar(out, in0, s1, s2, op0, op1), reads=rd, writes=[out])

                def stt(out, in0, sc, in1, op0, op1):
                    rd = [in0, in1] + ([] if isinstance(sc, (int, float)) else [sc])
                    S.add("dve", lambda e: e.scalar_tensor_tensor(out, in0, sc, in1, op0, op1), reads=rd, writes=[out])

                def cp(eng, out, in_):
                    S.add(eng, lambda e: e.tensor_copy(out, in_), reads=[in_], writes=[out])

                def memset(eng, ap, v):
                    S.add(eng, lambda e: e.memset(ap, v), writes=[ap])

                dma("sp", vecs[:, :], vecs_d, ("c", 0), writes=[vecs[:, :]])
                dma("sp", cT[:, :], cT_d, ("c", 1), writes=[cT[:, :]])
                dma("sp", bgate, bgate_d, ("c", 2), writes=[bgate])
                dma("pool", ident[:, :], ident_d, ("c", 3), writes=[ident[:, :]])
                memset("pool", ones_b[:, :], 1.0)
                memset("pool", ones_f[:, :], 1.0)
                memset("pool", sel0[:, :], 0.0)
                memset("pool", sel1[:, :], 0.0)
                memset("pool", sel0[0:64, :], 1.0)
                memset("pool", sel1[64:128, :], 1.0)
                memset("pool", eps_t[:, :], EPS)
                wm_views = [V[:, 0:8192], V[:, 8192:16384], KN[:, 0:8192]]
                for s in range(3):
                    dma("pool", wm_views[s].rearrange("p (j c) -> p j c", c=1024),
                        wmod_d[:, s * 1024:(s + 1) * 1024].rearrange("(j p) c -> p j c", p=128),
                        ("wm", s), writes=[wm_views[s]])
                dma("pool", w_in_a[:, :].rearrange("p (j c) -> p j c", c=640),
                    wina_d.rearrange("(j p) c -> p j c", p=128), ("w", 0), writes=[w_in_a[:, :]])
                dma("pool", w_ukv[:, :], wukv_d, ("w", 1), writes=[w_ukv[:, :]])
                dma("pool", w_uq[:, :].rearrange("p (j c) -> p j c", c=1024),
                    wuq_d.rearrange("(j p) c -> p j c", p=128), ("w", 2), writes=[w_uq[:, :]])

                if stop == "c":
                    S.closed = True
                act(e16[:, :], cT[:, :], AF.Exp, scale=-1.0)
                ts("dve", e16[:, :], e16[:, :], 1.0, None, ALU.add)
                S.add("dve", lambda e: e.reciprocal(e16[:, :], e16[:, :]), reads=[e16[:, :]], writes=[e16[:, :]])
                tt("dve", scb[:, :], cT[:, :], e16[:, :], ALU.mult)
                psmod = mm[0]
                for s in range(2):
                    for t in range(8):
                        col = (s * 8 + t) * 2
                        for j in range(8):
                            mmul(psmod[:, col:col + 2], wm_views[s][:, j * 1024 + t * 128:j * 1024 + (t + 1) * 128],
                                 scb[:, 2 * j:2 * j + 2], start=(j == 0), stop=(j == 7))
                for r in range(2):
                    tt("dve", modsb[:, r * 16:(r + 1) * 16], psmod[:, r:32:2], vecs[:, 21:37], ALU.add)
                    stt(Gm[:, r * 8:(r + 1) * 8], modsb[:, r * 16 + 8:r * 16 + 16], 1.0, vecs[:, 0:8], ALU.add, ALU.mult)
                    cp("dve", SH[:, r * 8:(r + 1) * 8], modsb[:, r * 16:r * 16 + 8])
                for half in range(2):
                    for j in range(8):
                        mmul(mm[1 + half][0:2, :], scb[:, 2 * j:2 * j + 2],
                             wm_views[2][:, j * 1024 + half * 512:j * 1024 + (half + 1) * 512],
                             start=(j == 0), stop=(j == 7))
                    tt("dve", gate_row[0:1, half * 512:(half + 1) * 512], mm[1 + half][0:1, :],
                       bgate[0:1, half * 512:(half + 1) * 512], ALU.add)
                for half in range(2):
                    mmul(mm[1 + half][:, :], ones_f[0:1, :], gate_row[0:1, half * 512:(half + 1) * 512])
                    cp("dve", gate_bc[:, half * 512:(half + 1) * 512], mm[1 + half][:, :])

                if stop == "p0":
                    S.closed = True
                gql = [vecs[:, 8:9], vecs[:, 9:10]]
                gkvl = vecs[:, 10:11]
                qgn, qgr, qgp = vecs[:, 11:12], vecs[:, 12:13], vecs[:, 13:14]
                kgn, kgr, kgp = vecs[:, 14:15], vecs[:, 15:16], vecs[:, 16:17]
                F_LN, F_RCQ, F_RCKV, F_T2, F_ROA, F_ROB, F_R0 = 0, 1, 2, 3, 4, 5, 6
                B_SQ0, B_SQ1, B_CQ0, B_CQ1, B_SQKV, B_CKV, B_SQKR, B_SQN = range(8)

                def rms_from(ps_bank, T, nfeat, dst):
                    act(FP(F_LN, T), ps_bank[:, 0:T], AF.Ln, scale=1.0 / nfeat, bias=eps_t[:, 0:1])
                    act(dst, FP(F_LN, T), AF.Exp, scale=-0.5)

                def load_tabs(tok0, T, key):
                    dma("sp", tabs[:, :].rearrange("p (a t) -> p a t", t=512)[:, :, 0:T],
                        tabs_d[:, :, tok0:tok0 + T], ("tab", 0), writes=[tabs[:, :]])

                def chunk_geom(ci):
                    T = 256 if ci == 0 else 512
                    tok0 = 0 if ci == 0 else CTX + (ci - 1) * 512
                    g0 = 0 if ci == 0 else 2 + (ci - 1) * 4
                    return T, T // 128, tok0, g0

                def xpath_stats(ci, tiles):
                    T, nt, tok0, g0 = chunk_geom(ci)
                    for i in tiles:
                        gi = g0 + i
                        sl = gi % 2
                        xi = xin[:, sl * D:(sl + 1) * D]
                        xsi = xs[:, sl * D:(sl + 1) * D]
                        src = ctx_d[i * 128:(i + 1) * 128, :] if ci == 0 else \
                            x_d[(ci - 1) * 512 + i * 128:(ci - 1) * 512 + (i + 1) * 128, :]
                        dma("sp", xi, src, ("x", sl), writes=[xi])
                        act(xsi, xi, AF.Square, accum=sstmp[:, gi % 4:gi % 4 + 1])
                        act(lntmp[:, gi % 4:gi % 4 + 1], sstmp[:, gi % 4:gi % 4 + 1], AF.Ln, scale=1.0 / D,
                            bias=eps_t[:, 0:1])
                        act(r_all[:, gi:gi + 1], lntmp[:, gi % 4:gi % 4 + 1], AF.Exp, scale=-0.5)
                        ts("dve", xsi, xi, r_all[:, gi:gi + 1], None, ALU.mult)

                def xpath_tr(ci, tiles):
                    T, nt, tok0, g0 = chunk_geom(ci)
                    for i in tiles:
                        sl = (g0 + i) % 2
                        xsi = xs[:, sl * D:(sl + 1) * D]
                        for j in range(8):
                            o = tp[j // 2][:, (j % 2) * 512 + i * 128:(j % 2) * 512 + (i + 1) * 128]
                            inn = xsi[:, j * 128:(j + 1) * 128]
                            S.add("pe", lambda e, o=o, inn=inn: e.transpose(o, inn, ident[:, :]),
                                  reads=[inn, ident[:, :]], writes=[o])

                def xpath_evac(ci):
                    T, nt, tok0, g0 = chunk_geom(ci)
                    rsel = 1 if ci == 0 else 0
                    for j in range(8):
                        src = tp[j // 2][:, (j % 2) * 512:(j % 2) * 512 + T]
                        dst = hT[:, j * 512:j * 512 + T]
                        g_ap = Gm[:, rsel * 8 + j:rsel * 8 + j + 1]
                        s_ap = SH[:, rsel * 8 + j:rsel * 8 + j + 1]
                        act(dst, src, AF.Identity, scale=g_ap, bias=s_ap)

                def halves(ci):
                    nt = chunk_geom(ci)[1]
                    return list(range(nt // 2)), list(range(nt // 2, nt))

                h0, h1 = halves(0)
                xpath_stats(0, h0)
                xpath_tr(0, h0)
                xpath_stats(0, h1)
                xpath_tr(0, h1)
                xpath_evac(0)
                for ci in range(9):
                    if stop == "p1_%d" % ci:
                        S.closed = True
                    T, nt, tok0, g0 = chunk_geom(ci)
                    load_tabs(tok0, T, 0)
                    nxt = ci + 1 < 9 and not S.closed
                    if nxt:
                        n0, n1 = halves(ci + 1)
                        xpath_stats(ci + 1, n0)

                    def umm(bank, col0):
                        for j in range(8):
                            mmul(bank[:, 0:T], w_in_a[:, j * 640 + col0:j * 640 + col0 + 128],
                                 hT[:, j * 512:j * 512 + T], start=(j == 0), stop=(j == 7))

                    cqn = [BP(B_CQ0, T), BP(B_CQ1, T)]
                    ckvn = BP(B_CKV, T)
                    cosT = tabs[:, 0:T]
                    sinT = tabs[:, 512:512 + T]
                    sqt = [B_SQN, B_SQKV]
                    umm(mm[0], 384)
                    umm(mm[1], 512)
                    umm(mm[2], 256)
                    if nxt:
                        xpath_tr(ci + 1, n0)
                        xpath_stats(ci + 1, n1)
                    ts("pool", cosT, cosT, kgr, None, ALU.mult)
                    ts("pool", sinT, sinT, kgp, None, ALU.mult)
                    tt("dve", FP(F_ROA, T), mm[0][:, 0:T], cosT, ALU.mult)
                    tt("dve", FP(F_T2, T), mm[1][:, 0:T], sinT, ALU.mult)
                    tt("pool", FP(F_ROA, T), FP(F_ROA, T), FP(F_T2, T), ALU.add)
                    act(BP(B_SQKR, T), mm[0][:, 0:T], AF.Square)
                    act(BP(B_SQKV, T), mm[2][:, 0:T], AF.Square)
                    mmul(mm[3][:, 0:T], ones_b[:, :], BP(B_SQKV, T))
                    rms_from(mm[3], T, 128, FP(F_RCKV, T))
                    stt(BP(B_CKV, T), mm[2][:, 0:T], gkvl, FP(F_RCKV, T), ALU.mult, ALU.mult)
                    umm(mm[0], 0)
                    umm(mm[1], 128)
                    if nxt:
                        xpath_tr(ci + 1, n1)
                        xpath_evac(ci + 1)
                    act(BP(B_SQ0, T), mm[0][:, 0:T], AF.Square)
                    act(BP(B_SQ1, T), mm[1][:, 0:T], AF.Square)
                    mmul(mm[3][:, 0:T], ones_b[:, :], BP(B_SQ0, T), start=True, stop=False)
                    mmul(mm[3][:, 0:T], ones_b[:, :], BP(B_SQ1, T), start=False, stop=True)
                    rms_from(mm[3], T, 256, FP(F_RCQ, T))
                    stt(BP(B_CQ0, T), mm[0][:, 0:T], gql[0], FP(F_RCQ, T), ALU.mult, ALU.mult)
                    stt(BP(B_CQ1, T), mm[1][:, 0:T], gql[1], FP(F_RCQ, T), ALU.mult, ALU.mult)
                    lnt = [F_LN, F_ROB]
                    for hp in range(2):
                        hs = [(2 * hp + hh, hh, (mm[2], mm[3]) if hh == 0 else (mm[0], mm[1])) for hh in range(2)]
                        for h, hh, (bk, bs) in hs:
                            mmul(bk[:, 0:T], w_ukv[:, h * 128:(h + 1) * 128], ckvn)
                        for h, hh, (bk, bs) in hs:
                            act(BP(sqt[hh], T), bk[:, 0:T], AF.Square)
                        for h, hh, (bk, bs) in hs:
                            mmul(bs[:, 0:T], ones_b[:, :], BP(sqt[hh], T), start=True, stop=False)
                            mmul(bs[:, 0:T], sels[hh][:, :], BP(B_SQKR, T), start=False, stop=True)
                        for h, hh, (bk, bs) in hs:
                            act(FP(lnt[hh], T), bs[:, 0:T], AF.Ln, scale=1.0 / 192, bias=eps_t[:, 0:1])
                        for h, hh, (bk, bs) in hs:
                            act(FP(F_R0 + hh, T), FP(lnt[hh], T), AF.Exp, scale=-0.5)
                        for h, hh, (bk, bs) in hs:
                            rows = slice(hh * 64, hh * 64 + 64)
                            rk = FP(F_R0 + hh, T)
                            stt(KN[:, h * NTOK + tok0:h * NTOK + tok0 + T], bk[:, 0:T], kgn, rk, ALU.mult, ALU.mult)
                            tt("pool", KR[rows, hp * NTOK + tok0:hp * NTOK + tok0 + T],
                               fpool[rows, F_ROA * 512:F_ROA * 512 + T],
                               fpool[rows, (F_R0 + hh) * 512:(F_R0 + hh) * 512 + T], ALU.mult)
                    for i in range(nt):
                        kt = tok0 // 128 + i
                        bank = mm[i % 2]
                        mmul(bank[:, :], ckvn[:, i * 128:(i + 1) * 128], w_ukv[:, 512:1024])
                        if i % 2 == 0:
                            cp("dve", V[:, kt * 512:(kt + 1) * 512], bank[:, :])
                        else:
                            act(V[:, kt * 512:(kt + 1) * 512], bank[:, :], AF.Copy)
                    if ci == 0:
                        continue
                    q0 = (ci - 1) * 512
                    load_tabs(tok0, T, 0)
                    ts("pool", cosT, cosT, qgr, None, ALU.mult)
                    ts("pool", sinT, sinT, qgp, None, ALU.mult)
                    ro = [F_ROA, F_ROB]
                    sqr = [B_SQ0, B_SQ1]
                    for p in range(2):
                        for kk in range(2):
                            mmul(mm[0][:, :], w_uq[:, kk * 1024 + 512 + p * 128:kk * 1024 + 512 + (p + 1) * 128], cqn[kk],
                                 start=(kk == 0), stop=(kk == 1))
                        for kk in range(2):
                            mmul(mm[1][:, :], w_uq[:, kk * 1024 + 768 + p * 128:kk * 1024 + 768 + (p + 1) * 128], cqn[kk],
                                 start=(kk == 0), stop=(kk == 1))
                        tt("dve", FP(ro[p]), mm[0][:, :], cosT, ALU.mult)
                        tt("dve", FP(F_T2), mm[1][:, :], sinT, ALU.mult)
                        tt("pool", FP(ro[p]), FP(ro[p]), FP(F_T2), ALU.add)
                        act(BP(sqr[p]), mm[0][:, :], AF.Square)
                    b0 = 4 * (ci - 1)
                    pv = lambda a: a.rearrange("p (b i) -> p b i", i=128)
                    lnq = [F_LN, F_T2]
                    for hp in range(2):
                        hs = [(2 * hp + hh, hh, (mm[2], mm[3]) if hh == 0 else (mm[0], mm[1])) for hh in range(2)]
                        for h, hh, (bk, bs) in hs:
                            for kk in range(2):
                                mmul(bk[:, :], w_uq[:, kk * 1024 + h * 128:kk * 1024 + (h + 1) * 128], cqn[kk],
                                     start=(kk == 0), stop=(kk == 1))
                        for h, hh, (bk, bs) in hs:
                            act(BP(sqt[hh]), bk[:, :], AF.Square)
                        for h, hh, (bk, bs) in hs:
                            mmul(bs[:, :], ones_b[:, :], BP(sqt[hh]), start=True, stop=False)
                            mmul(bs[:, :], sels[hh][:, :], BP(sqr[hp]), start=False, stop=True)
                        for h, hh, (bk, bs) in hs:
                            act(FP(lnq[hh]), bs[:, :], AF.Ln, scale=1.0 / 192, bias=eps_t[:, 0:1])
                        for h, hh, (bk, bs) in hs:
                            act(FP(F_R0 + hh), FP(lnq[hh]), AF.Exp, scale=-0.5)
                        for h, hh, (bk, bs) in hs:
                            rows = slice(hh * 64, hh * 64 + 64)
                            rq = FP(F_R0 + hh)
                            dq = QN[:, h * L:(h + 1) * L].rearrange("p (i b) -> p b i", b=32)[:, b0:b0 + 4, :]
                            stt(dq, pv(bk[:, :]), qgn, pv(rq), ALU.mult, ALU.mult)
                            dr = QR[rows, hp * L:(hp + 1) * L].rearrange("p (i b) -> p b i", b=32)[:, b0:b0 + 4, :]
                            tt("pool", dr, pv(fpool[rows, ro[hp] * 512:(ro[hp] + 1) * 512]),
                               pv(fpool[rows, (F_R0 + hh) * 512:(F_R0 + hh + 1) * 512]), ALU.mult)

                prev = _run_block(nc, es, S, prev)
                if stop is not None and (stop in ("a1", "c", "p0") or stop.startswith("p1_")):
                    return nc

            with ExitStack() as es2:
                sT = [es2.enter_context(nc.psum_tensor("sT%d" % i, [128, 1024], F32)) for i in range(2)]
                Ob = [es2.enter_context(nc.psum_tensor("Ob%d" % i, [128, 512], F32)) for i in range(2)]
                smb = es2.enter_context(nc.psum_tensor("smb", [128, 512], F32))
                S = Sched("a2")
                SCALE = 192.0 ** -0.5
                NG = NKT // 2

                def mmul2(out, lhsT, rhs, start=True, stop=True):
                    S.add("pe", lambda e: e.matmul(out, lhsT, rhs, start=start, stop=stop),
                          reads=[lhsT, rhs] + ([] if start else [out]), writes=[out])

                stage = [bpool[:, 6 * 512:7 * 512], bpool[:, 7 * 512:8 * 512]]
                for t_ in stage:
                    S.add("pool", lambda e, t_=t_: e.memset(t_, 0.0), writes=[t_])

                def stage_copy(itn):
                    cn, hn = itn // 4, itn % 4
                    rw = slice((hn % 2) * 64, (hn % 2) * 64 + 64)
                    src = QR[rw, (hn // 2) * L + cn * 512:(hn // 2) * L + (cn + 1) * 512]
                    dst = bpool[rw, (6 + hn % 2) * 512:(7 + hn % 2) * 512]
                    S.add("pool", lambda e: e.tensor_copy(dst, src), reads=[src], writes=[dst])

                pending = None
                late = []
                it = 0
                for c in range(8):
                    for h in range(4):
                        hp, hh = h // 2, h % 2
                        ob = Ob[it % 2]
                        sset = (it % 2) * 3
                        accs = [fpool[:, (sset + k) * 512:(sset + k + 1) * 512] for k in (0, 1, 2)]
                        lnb = fpool[:, 6 * 512:7 * 512]
                        qn = QN[:, h * L + c * 512:h * L + (c + 1) * 512]
                        qr = stage[hh]

                        def qk(g, h=h, hp=hp, qn=qn, qr=qr):
                            for u in range(2):
                                j = 2 * g + u
                                o = sT[g % 2][:, u * 512:(u + 1) * 512]
                                mmul2(o, KN[:, h * NTOK + j * 128:h * NTOK + (j + 1) * 128], qn, start=True, stop=False)
                                mmul2(o, KR[:, hp * NTOK + j * 128:hp * NTOK + (j + 1) * 128], qr,
                                      start=False, stop=True)

                        if it == 0:
                            stage_copy(0)
                        qk(0)
                        qk(1)
                        if it + 1 < 32:
                            stage_copy(it + 1)
                        seen = [False, False, False]
                        for g in range(NG):
                            pt = bpool[:, (g % 3) * 1024:(g % 3 + 1) * 1024]
                            st = sT[g % 2][:, :]
                            S.add("act", lambda e, pt=pt, st=st: e.activation(pt, st, AF.Exp, scale=SCALE),
                                  reads=[st], writes=[pt])
                            for u in range(2):
                                if u == 1 and g % 2 == 1:
                                    ai, eng = 2, "pool"
                                else:
                                    ai, eng = u, "dve"
                                accb = accs[ai]
                                pth = pt[:, u * 512:(u + 1) * 512]
                                if not seen[ai]:
                                    seen[ai] = True
                                    S.add(eng, lambda e, accb=accb, pth=pth: e.tensor_copy(accb, pth), reads=[pth], writes=[accb])
                                else:
                                    S.add(eng, lambda e, accb=accb, pth=pth: e.tensor_tensor(accb, accb, pth, ALU.add),
                                          reads=[accb, pth], writes=[accb])
                            for u in range(2):
                                j = 2 * g + u
                                mmul2(ob[:, :], V[:, j * 512 + h * 128:j * 512 + (h + 1) * 128],
                                      pt[:, u * 512:(u + 1) * 512], start=(j == 0), stop=(j == NKT - 1))
                            if g + 2 < NG:
                                qk(g + 2)
                            if g == 3 and pending is not None:
                                late = pending()
                                pending = None
                            elif g > 3 and late:
                                late.pop(0)()

                        def fin(ob=ob, accs=accs, qn=qn, lnb=lnb):
                            S.add("dve", lambda e: e.tensor_tensor(accs[0], accs[0], accs[1], ALU.add),
                                  reads=[accs[0], accs[1]], writes=[accs[0]])
                            S.add("dve", lambda e: e.tensor_tensor(accs[1], accs[0], accs[2], ALU.add),
                                  reads=[accs[0], accs[2]], writes=[accs[1]])
                            todo = [lambda: None, lambda: mmul2(smb[:, :], ones_f[:, :], accs[1])]
                            for q4 in range(4):
                                a_, b_ = lnb[:, q4 * 128:(q4 + 1) * 128], smb[:, q4 * 128:(q4 + 1) * 128]
                                todo.append(lambda a_=a_, b_=b_: S.add("dve", lambda e: e.reciprocal(a_, b_),
                                                                       reads=[b_], writes=[a_]))
                            todo.append(lambda: S.add("dve", lambda e: e.tensor_tensor(qn, ob[:, :], lnb, ALU.mult),
                                                      reads=[ob[:, :], lnb], writes=[qn]))
                            return todo

                        pending = fin
                        it += 1
                for fn_ in pending():
                    fn_()
                prev = _run_block(nc, es, S, prev)
                if stop == "a2":
                    return nc

        with ExitStack() as esB:
            sbB = lambda n, shp, dt: esB.enter_context(nc.sbuf_tensor(n, shp, dt))
            wb = [sbB("w_in_b%d_sb" % i, [128, 8 * 512], BF16) for i in range(3)]
            w_out = sbB("w_out_sb", [128, 8 * D], BF16)
            w_pool = sbB("w_pool_sb", [128, 512], BF16)
            xin = sbB("xinB", [128, 4 * D], F32)
            xs = sbB("xsB", [128, 4 * D], BF16)
            hT = sbB("hTB", [128, 3 * 4096], BF16)
            Lh = sbB("Lh", [128, 128], BF16)
            sg = sbB("sg", [128, 8 * 512], BF16)
            U = sbB("U", [128, 4 * 528], F32)
            tA0 = sbB("tA", [128, 528], F32)
            tB0 = sbB("tB", [128, 528], F32)
            tC = sbB("tC", [128, 528], F32)
            tD = sbB("tD", [128, 528], F32)
            t8 = sbB("t8", [128, 8], F32)
            pooled = sbB("pooled", [128, 4 * 512], BF16)
            br = sbB("br", [128, 8 * 512], BF16)
            xres = sbB("xres", [128, 4 * D], F32)
            ot = sbB("ot", [128, 4 * D], F32)
            invc = sbB("invc_sb", [128, 64], F32)
            tpb = [esB.enter_context(nc.psum_tensor("tpB%d" % i, [128, 512], BF16)) for i in range(2)]
            mm = [esB.enter_context(nc.psum_tensor("mmB%d" % i, [128, 512], F32)) for i in range(3)]
            hl = esB.enter_context(nc.psum_tensor("hlB", [128, 512], F32))
            yb = [esB.enter_context(nc.psum_tensor("yB%d" % i, [128, 512], F32)) for i in range(2)]
            S = Sched("b")

            def dma(q, out, in_, key, reads=(), writes=()):
                S.add(q, lambda e: e.dma_start(out=out, in_=in_), reads=reads, writes=writes, dma=key)

            def mmul(out, lhsT, rhs, start=True, stop=True):
                S.add("pe", lambda e: e.matmul(out, lhsT, rhs, start=start, stop=stop),
                      reads=[lhsT, rhs] + ([] if start else [out]), writes=[out])

            def tt(eng, out, in0, in1, op):
                S.add(eng, lambda e: e.tensor_tensor(out, in0, in1, op), reads=[in0, in1], writes=[out])

            def stt(out, in0, sc, in1, op0, op1):
                rd = [in0, in1] + ([] if isinstance(sc, (int, float)) else [sc])
                S.add("dve", lambda e: e.scalar_tensor_tensor(out, in0, sc, in1, op0, op1), reads=rd, writes=[out])

            for wi in (1, 0, 2):
                dma("pool", wb[wi][:, :].rearrange("p (j c) -> p j c", c=512),
                    winb_d[:, wi * 512:(wi + 1) * 512].rearrange("(j p) c -> p j c", p=128), ("wb", wi),
                    writes=[wb[wi][:, :]])
            dma("pool", w_pool[:, :].rearrange("p (g d) -> p g d", d=128),
                wpool_d.rearrange("g c d -> c g d"), ("w", 1), writes=[w_pool[:, :]])
            dma("pool", w_out[:, :].rearrange("p (j c) -> p j c", c=D),
                wout_d.rearrange("(j p) c -> p j c", p=128), ("w", 2), writes=[w_out[:, :]])
            dma("sp", invc[:, :], invc_d, ("c", 0), writes=[invc[:, :]])
            for g in range(4):
                S.add("pool", lambda e, g=g: e.memset(U[:, g * 528:g * 528 + 8], 0.0), writes=[U[:, g * 528:g * 528 + 8]])
            S.add("pool", lambda e: e.memset(Lh[:, :], 0.0), writes=[Lh[:, :]])

            pscale = [vecs[:, 17 + g:18 + g] for g in range(4)]
            mmrot = [0]

            def nextmm():
                b = mm[mmrot[0] % 3]
                mmrot[0] += 1
                return b

            def stageA(c):
                hc = hT[:, (c % 3) * 4096:(c % 3 + 1) * 4096]
                for i in range(4):
                    gi = c * 4 + i
                    sl = gi % 4
                    xi = xin[:, sl * D:(sl + 1) * D]
                    dma("sp", xi, x_d[gi * 128:(gi + 1) * 128, :], ("x", sl), writes=[xi])
                    xsi = xs[:, i * D:(i + 1) * D]
                    rr = r_all[:, 2 + gi:3 + gi]
                    S.add("dve", lambda e, xsi=xsi, xi=xi, rr=rr: e.tensor_scalar(xsi, xi, rr, None, ALU.mult),
                          reads=[xi, rr], writes=[xsi])
                for j in range(8):
                    reg = j % 2
                    for i in range(4):
                        o = tpb[reg][:, i * 128:(i + 1) * 128]
                        inn = xs[:, i * D + j * 128:i * D + (j + 1) * 128]
                        S.add("pe", lambda e, o=o, inn=inn: e.transpose(o, inn, ident[:, :]),
                              reads=[inn, ident[:, :]], writes=[o])
                    src = tpb[reg][:, :]
                    dst = hc[:, j * 512:(j + 1) * 512]
                    g_ap = Gm[:, j:j + 1]
                    s_ap = SH[:, j:j + 1]
                    S.add("act", lambda e, dst=dst, src=src, g_ap=g_ap, s_ap=s_ap:
                          e.activation(dst, src, AF.Identity, scale=g_ap, bias=s_ap),
                          reads=[src, g_ap, s_ap], writes=[dst])

            WIN = [2, 4, 8, 16]

            def stageB_a(c):
                hc = hT[:, (c % 3) * 4096:(c % 3 + 1) * 4096]
                hn = hT[:, ((c + 1) % 3) * 4096:((c + 1) % 3 + 1) * 4096]
                if c < 7:
                    a = Lh[:, :].rearrange("p (j t) -> p j t", t=16)[:, :, 8:16]
                    b = hn.rearrange("p (j t) -> p j t", t=512)[:, :, 0:8]
                    S.add("pool", lambda e, a=a, b=b: e.tensor_copy(a, b), reads=[b], writes=[a])
                for g in range(4):
                    col0 = g * 128
                    bank = nextmm()
                    for j in range(8):
                        mmul(bank[:, :], wb[1][:, j * 512 + col0:j * 512 + col0 + 128], hc[:, j * 512:(j + 1) * 512],
                             start=(j == 0), stop=(j == 7))
                    for j in range(8):
                        mmul(hl[:, g * 16:g * 16 + 16], wb[1][:, j * 512 + col0:j * 512 + col0 + 128],
                             Lh[:, j * 16:(j + 1) * 16], start=(j == 0), stop=(j == 7))
                    ug = U[:, g * 528 + 8:g * 528 + 520]
                    S.add("act", lambda e, ug=ug, bank=bank: e.activation(ug, bank[:, :], AF.Copy),
                          reads=[bank[:, :]], writes=[ug])
                    if c > 0:
                        a, b = U[:, g * 528:g * 528 + 8], hl[:, g * 16:g * 16 + 8]
                        S.add("dve", lambda e, a=a, b=b: e.tensor_copy(a, b), reads=[b], writes=[a])
                    if c < 7:
                        a, b = U[:, g * 528 + 520:g * 528 + 528], hl[:, g * 16 + 8:g * 16 + 16]
                        S.add("dve", lambda e, a=a, b=b: e.tensor_copy(a, b), reads=[b], writes=[a])
                    else:
                        a = U[:, g * 528 + 520:g * 528 + 528]
                        S.add("pool", lambda e, a=a: e.memset(a, 0.0), writes=[a])
            def stageB(c):
                hc = hT[:, (c % 3) * 4096:(c % 3 + 1) * 4096]
                for g in range(4):
                    w = WIN[g]
                    u0 = g * 528
                    eng = "dve" if g % 2 == 0 else "pool"
                    tA, tB = (tA0, tB0) if g % 2 == 0 else (tC, tD)
                    tt(eng, tA[:, 0:527], U[:, u0:u0 + 527], U[:, u0 + 1:u0 + 528], ALU.add)
                    src = tA
                    if w >= 4:
                        tt(eng, tB[:, 0:525], tA[:, 0:525], tA[:, 2:527], ALU.add)
                        src = tB
                    if w >= 8:
                        tt(eng, tA[:, 0:521], tB[:, 0:521], tB[:, 4:525], ALU.add)
                        src = tA
                    if w >= 16:
                        tt(eng, tB[:, 0:513], tA[:, 0:513], tA[:, 8:521], ALU.add)
                        src = tB
                    s0 = 8 - w // 2
                    stt(pooled[:, g * 512:(g + 1) * 512], src[:, s0:s0 + 512], 1.0 / w, U[:, u0 + 8:u0 + 520],
                        ALU.mult, ALU.subtract)
                    if c == 0:
                        tt("dve", t8[:, :], src[:, s0:s0 + 8], invc[:, g * 8:(g + 1) * 8], ALU.mult)
                        tt("dve", pooled[:, g * 512:g * 512 + 8], t8[:, :], U[:, u0 + 8:u0 + 16], ALU.subtract)
                    if c == 7:
                        tt("dve", t8[:, :], src[:, s0 + 504:s0 + 512], invc[:, 32 + g * 8:32 + (g + 1) * 8], ALU.mult)
                        tt("dve", pooled[:, g * 512 + 504:g * 512 + 512], t8[:, :], U[:, u0 + 512:u0 + 520], ALU.subtract)
                for k in range(8):
                    wk = wb[0] if k < 4 else wb[2]
                    col0 = (k % 4) * 128
                    bank = nextmm()
                    for j in range(8):
                        mmul(bank[:, :], wk[:, j * 512 + col0:j * 512 + col0 + 128], hc[:, j * 512:(j + 1) * 512],
                             start=(j == 0), stop=(j == 7))
                    dst = sg[:, k * 512:(k + 1) * 512]
                    S.add("act", lambda e, dst=dst, bank=bank: e.activation(dst, bank[:, :], AF.Silu),
                          reads=[bank[:, :]], writes=[dst])
                for g in range(4):
                    bank = nextmm()
                    mmul(bank[:, :], w_pool[:, g * 128:(g + 1) * 128], pooled[:, g * 512:(g + 1) * 512])
                    stt(br[:, (4 + g) * 512:(5 + g) * 512], bank[:, :], pscale[g], sg[:, (4 + g) * 512:(5 + g) * 512],
                        ALU.mult, ALU.mult)
                for h in range(4):
                    tt("pool", br[:, h * 512:(h + 1) * 512], sg[:, h * 512:(h + 1) * 512],
                       QN[:, h * L + c * 512:h * L + (c + 1) * 512], ALU.mult)
                if c < 7:
                    a = Lh[:, :].rearrange("p (j t) -> p j t", t=16)[:, :, 0:8]
                    b = hc.rearrange("p (j t) -> p j t", t=512)[:, :, 504:512]
                    S.add("pool", lambda e, a=a, b=b: e.tensor_copy(a, b), reads=[b], writes=[a])
            def stageB2(c):
                for i in range(4):
                    gi = c * 4 + i
                    xr = xres[:, i * D:(i + 1) * D]
                    dma("sp", xr, x_d[gi * 128:(gi + 1) * 128, :], ("xr", i), writes=[xr])
                for i in range(4):
                    gi = c * 4 + i
                    sl = i
                    xr = xres[:, sl * D:(sl + 1) * D]
                    oo = ot[:, sl * D:(sl + 1) * D]
                    for half in range(2):
                        for kk in range(8):
                            mmul(yb[half][:, :], br[:, kk * 512 + i * 128:kk * 512 + (i + 1) * 128],
                                 w_out[:, kk * D + half * 512:kk * D + (half + 1) * 512], start=(kk == 0), stop=(kk == 7))
                        tt("dve", oo[:, half * 512:(half + 1) * 512], yb[half][:, :], xr[:, half * 512:(half + 1) * 512],
                           ALU.add)
                    dma("sp", out_d[gi * 128:(gi + 1) * 128, :], oo, ("o", sl), reads=[oo])

            stageA(0)
            stageA(1)
            for c in range(8):
                stageB_a(c)
                if c + 2 < 8:
                    stageA(c + 2)
                stageB(c)
                if c == 0:
                    for kk in range(8):
                        wv = w_out[:, kk * D:(kk + 1) * D]
                        tt("dve" if kk % 2 == 0 else "pool", wv, wv, gate_bc[:, :], ALU.mult)
                stageB2(c)
            prev = _run_block(nc, es, S, prev, final_waits=True)
    return nc


def _perm64():
    d = np.arange(64)
    return np.where((d % 32) < 16, d + 16, d - 16)


def _host_consts():
    perm = _perm64()
    sign = np.where((np.arange(64) % 32) < 16, -1.0, 1.0).astype(np.float32)
    t = np.arange(L)
    row = (t // 64).astype(np.float32)
    col = (t % 64).astype(np.float32)
    inv = (np.float32(10000.0) ** (-np.arange(16, dtype=np.float32) / np.float32(16))).astype(np.float32)
    ang_r = row[:, None] * inv[None, :]
    ang_c = col[:, None] * inv[None, :]
    ang = np.concatenate([ang_r, ang_r, ang_c, ang_c], axis=-1).astype(np.float32)
    cos = np.cos(ang).astype(np.float32)
    sin = np.sin(ang).astype(np.float32)
    tabs = np.zeros((128, 2, NTOK), np.float32)
    tabs[:, 0, :CTX] = 1.0
    tabs[0:64, 0, CTX:] = cos.T
    tabs[64:128, 0, CTX:] = cos.T
    tabs[0:64, 1, CTX:] = (sin * sign[None, :]).T
    tabs[64:128, 1, CTX:] = (sin * sign[None, :]).T
    invc = np.zeros((128, 64), np.float32)
    for g, w in enumerate((2, 4, 8, 16)):
        for k in range(8):
            tt_ = k
            lo = max(tt_ - w // 2, 0)
            hi = min(tt_ - w // 2 + w, L)
            invc[:, g * 8 + k] = 1.0 / (hi - lo)
            tt_ = L - 8 + k
            lo = max(tt_ - w // 2, 0)
            hi = min(tt_ - w // 2 + w, L)
            invc[:, 32 + g * 8 + k] = 1.0 / (hi - lo)
    ident = np.eye(128, dtype=np.float32)
    return tabs, invc, ident


_NC = None


def kernel(x, c, ctx, c_ctx, w_mod, b_mod, norm_g, w_in, q_lora_g, w_uq, kv_lora_g, w_ukv,
           q_norm_g, k_norm_g, w_pool, pool_scale, w_out):
    global _NC
    f = lambda a: np.ascontiguousarray(np.asarray(a, dtype=np.float32))
    x, c, ctx, c_ctx = f(x), f(c), f(ctx), f(c_ctx)
    w_mod, b_mod, norm_g, w_in = f(w_mod)[0], f(b_mod)[0], f(norm_g)[0], f(w_in)[0]
    q_lora_g, w_uq, kv_lora_g, w_ukv = f(q_lora_g)[0], f(w_uq)[0], f(kv_lora_g)[0], f(w_ukv)[0]
    q_norm_g, k_norm_g, w_pool, pool_scale, w_out = f(q_norm_g)[0], f(k_norm_g)[0], f(w_pool)[0], f(pool_scale)[0], f(w_out)[0]
    perm = _perm64()
    tabs, invc, ident = _host_consts()
    kr = w_in[:, 384:448]
    krp = kr[:, perm]
    w_in_a = f(np.concatenate([w_in[:, 0:384], kr, kr, krp, krp], axis=1))
    w_in_b = f(w_in[:, 448:1984])
    nope = [w_uq[:, h * 192:h * 192 + 128] for h in range(4)]
    rope = [w_uq[:, h * 192 + 128:h * 192 + 192] for h in range(4)]
    ropep = [r[:, perm] for r in rope]
    w_uq_x = f(np.concatenate(nope + rope + ropep, axis=1))
    w_ukv_x = f(np.concatenate([w_ukv[:, h * 256:h * 256 + 128] for h in range(4)] +
                               [w_ukv[:, h * 256 + 128:h * 256 + 256] for h in range(4)], axis=1))
    vecs = np.zeros((128, NV), np.float32)
    vecs[:, 0:8] = norm_g.reshape(8, 128).T
    vecs[:, 8:10] = q_lora_g.reshape(2, 128).T
    vecs[:, 10] = kv_lora_g
    for base, g in ((11, q_norm_g), (14, k_norm_g)):
        vecs[:, base] = g[0:128]
        gr = g[128:192]
        vecs[:, base + 1] = np.concatenate([gr, gr])
        vecs[:, base + 2] = np.concatenate([gr[perm], gr[perm]])
    vecs[:, 17:21] = pool_scale.reshape(4, 128).T
    vecs[:, 21:45] = b_mod.reshape(24, 128).T
    bgate = f(b_mod[2048:3072].reshape(1, D))
    if _NC is None:
        _NC = build_nc(stop=os.environ.get("KSTOP"))
    in_maps = []
    for b in range(8):
        cvec = np.stack([c[b], c_ctx], axis=0)
        cT = f(cvec.reshape(2, 8, 128).transpose(2, 1, 0).reshape(128, 16))
        in_maps.append({
            "x": x[b], "ctx": ctx[b], "cT": cT, "vecs": vecs, "bgate": bgate, "w_mod": w_mod,
            "w_in_a": w_in_a, "w_in_b": w_in_b, "w_uq_x": w_uq_x, "w_ukv_x": w_ukv_x,
            "w_pool": w_pool, "w_out": w_out, "tabs": tabs, "ident": ident, "invc": invc,
        })
    res = run_bass_kernel_spmd(_NC, in_maps, core_ids=list(range(8)))
    return np.stack([np.asarray(r["out"], dtype=np.float32) for r in res.results], axis=0)
```

```python
import os
import numpy as np
from contextlib import ExitStack
import concourse.bass as bass
import concourse.mybir as mybir
from concourse.bass_utils import run_bass_kernel_spmd

F32 = mybir.dt.float32
BF16 = mybir.dt.bfloat16
AF = mybir.ActivationFunctionType
ALU = mybir.AluOpType

D = 1024
L = 4096
CTX = 256
NTOK = L + CTX
NKT = NTOK // 128
EPS = 1e-6
NV = 45
SAME_ENGINE_SYNC = {"act": True, "dve": True, "pool": True, "pe": False, "sp": True}
_DSZ = {F32: 4, BF16: 2}


class _Op:
    __slots__ = ("eng", "fn", "deps", "dma", "sig")

    def __init__(self, eng, fn, deps, dma):
        self.eng, self.fn, self.deps, self.dma, self.sig = eng, fn, deps, dma, None


class Sched:
    def __init__(self, name):
        self.name = name
        self.ops = []
        self.acc = {}
        self.final = {}
        self.closed = False

    @staticmethod
    def box(ap):
        t = ap.tensor
        pstep = 1
        for s in list(t.shape)[1:]:
            pstep *= int(s)
        off = int(ap.offset)
        dims = list(ap.ap)
        p0 = off // pstep
        f0 = off % pstep
        ext = 0
        for st, cn in dims[1:]:
            ext += abs(int(st)) * (int(cn) - 1)
        p1 = p0 + int(dims[0][1])
        if type(t).__name__.startswith("PSum"):
            return (t.name, (p0 // 32) * 32, ((p1 + 31) // 32) * 32, 0, pstep)
        return (t.name, p0, p1, f0, f0 + ext + 1)

    def add(self, eng, fn, reads=(), writes=(), dma=None):
        if self.closed:
            return -1
        idx = len(self.ops)
        mx = os.environ.get("KMAXOPS")
        if mx is not None and self.name == "a1":
            if idx >= int(mx):
                self.closed = True
                return -1
            import traceback
            self.dbg = getattr(self, "dbg", [])
            self.dbg.append((eng, [f.lineno for f in traceback.extract_stack(limit=5)[:-1]]))
        deps = set()
        sk = dma if dma is not None else eng
        for ap in reads:
            nm, p0, p1, f0, f1 = self.box(ap)
            lst = self.acc.setdefault(nm, [])
            found = False
            is_psum = type(ap.tensor).__name__.startswith("PSum")
            for e in lst:
                if e[0] < p1 and p0 < e[1] and e[2] < f1 and f0 < e[3]:
                    if e[5] or (is_psum and e[6] != sk):
                        deps.add(e[4])
                    elif (not found) and e[6] == sk and e[0] == p0 and e[1] == p1 and e[2] == f0 and e[3] == f1:
                        e[4] = idx
                        found = True
            if not found:
                lst.append([p0, p1, f0, f1, idx, False, sk])
        for ap in writes:
            nm, p0, p1, f0, f1 = self.box(ap)
            lst = self.acc.setdefault(nm, [])
            keep = []
            for e in lst:
                if e[0] < p1 and p0 < e[1] and e[2] < f1 and f0 < e[3]:
                    deps.add(e[4])
                    if e[0] >= p0 and e[1] <= p1 and e[2] >= f0 and e[3] <= f1:
                        continue
                keep.append(e)
            keep.append([p0, p1, f0, f1, idx, True, sk])
            self.acc[nm] = keep
        deps.discard(idx)
        self.ops.append(_Op(eng, fn, sorted(deps), dma))
        return idx

    @staticmethod
    def _needs(p, o):
        if p.dma is not None or o.dma is not None:
            return True
        if p.eng != o.eng:
            return True
        return SAME_ENGINE_SYNC.get(o.eng, True)

    def finalize(self):
        n = len(self.ops)
        needed = [False] * n
        for o in self.ops:
            for d in o.deps:
                if self._needs(self.ops[d], o):
                    needed[d] = True
        cnt = {}
        last = {}
        for i, o in enumerate(self.ops):
            last[o.eng] = i
        for i, o in enumerate(self.ops):
            if o.dma is not None:
                cnt[o.dma] = cnt.get(o.dma, 0) + 16
                o.sig = (o.dma, cnt[o.dma])
            elif needed[i] or last[o.eng] == i:
                cnt[o.eng] = cnt.get(o.eng, 0) + 1
                o.sig = (o.eng, cnt[o.eng])
        self.final = dict(cnt)
        return list(cnt.keys())

    def emit_engine(self, engname, e, sems, prologue=()):
        waited = {}
        for sem, val in prologue:
            e.wait_ge(sem, val)
        for o in self.ops:
            if o.eng != engname:
                continue
            w = {}
            for d in o.deps:
                p = self.ops[d]
                if p.sig is None or not self._needs(p, o):
                    continue
                k, v = p.sig
                if waited.get(k, 0) >= v:
                    continue
                if w.get(k, 0) < v:
                    w[k] = v
            for k, v in w.items():
                e.wait_ge(sems[k], v)
                waited[k] = v
            ins = o.fn(e)
            if o.sig is not None:
                ins.then_inc(sems[o.sig[0]], 16 if o.dma is not None else 1)


def _run_block(nc, es, sched, prev, final_waits=False):
    keys = sched.finalize()
    sems = {}
    for k in keys:
        nm = "s_%s_%s" % (sched.name, "_".join(str(x) for x in (k if isinstance(k, tuple) else (k,))))
        sems[k] = es.enter_context(nc.semaphore(nm))
    with nc.Block() as block:
        def mk(engname, with_final):
            def body(e):
                sched.emit_engine(engname, e, sems, prologue=prev)
                if with_final:
                    for k, v in sched.final.items():
                        e.wait_ge(sems[k], v)
            return body

        block.tensor(mk("pe", False))
        block.scalar(mk("act", False))
        block.vector(mk("dve", False))
        block.gpsimd(mk("pool", False))
        block.sync(mk("sp", final_waits))
    return [(sems[k], v) for k, v in sched.final.items()]


def build_nc(stop=None):
    nc = bass.Bass("TRN2", target_bir_lowering=False)
    dram = lambda n, shp: nc.dram_tensor(n, shp, F32, kind="ExternalInput").ap()
    x_d = dram("x", [L, D])
    ctx_d = dram("ctx", [CTX, D])
    cT_d = dram("cT", [128, 16])
    vecs_d = dram("vecs", [128, NV])
    bgate_d = dram("bgate", [1, D])
    wmod_d = dram("w_mod", [D, 3 * D])
    wina_d = dram("w_in_a", [D, 640])
    winb_d = dram("w_in_b", [D, 1536])
    wuq_d = dram("w_uq_x", [256, 1024])
    wukv_d = dram("w_ukv_x", [128, 1024])
    wpool_d = dram("w_pool", [4, 128, 128])
    wout_d = dram("w_out", [D, D])
    tabs_d = dram("tabs", [128, 2, NTOK])
    ident_d = dram("ident", [128, 128])
    invc_d = dram("invc", [128, 64])
    out_d = nc.dram_tensor("out", [L, D], F32, kind="ExternalOutput").ap()

    with ExitStack() as es:
        sb = lambda n, shp, dt: es.enter_context(nc.sbuf_tensor(n, shp, dt))
        QN = sb("QN", [128, 4 * L], BF16)
        gate_bc = sb("gate_bc", [128, D], F32)
        r_all = sb("r_all", [128, 36], F32)
        vecs = sb("vecs_sb", [128, NV], F32)
        Gm = sb("Gm", [128, 16], F32)
        SH = sb("SH", [128, 16], F32)
        ident = sb("ident_sb", [128, 128], BF16)
        ones_b = sb("ones_b", [128, 128], BF16)
        sel0 = sb("sel0", [128, 128], BF16)
        sel1 = sb("sel1", [128, 128], BF16)
        ones_f = sb("ones_f", [128, 128], F32)
        eps_t = sb("eps_t", [128, 1], F32)
        sels = [sel0, sel1]

        prev = []
        with ExitStack() as esA:
            sbA = lambda n, shp, dt: esA.enter_context(nc.sbuf_tensor(n, shp, dt))
            QR = sbA("QR", [128, 2 * L], BF16)
            KN = sbA("KN", [128, 4 * NTOK], BF16)
            KR = sbA("KR", [128, 2 * NTOK], BF16)
            V = sbA("V", [128, NKT * 512], BF16)
            w_in_a = sbA("w_in_a_sb", [128, 8 * 640], BF16)
            w_uq = sbA("w_uq_sb", [128, 2 * 1024], BF16)
            w_ukv = sbA("w_ukv_sb", [128, 1024], BF16)
            xin = sbA("xin", [128, 2 * D], F32)
            xs = sbA("xs", [128, 2 * D], BF16)
            hT = sbA("hT", [128, 8 * 512], BF16)
            tabs = sbA("tabs_sb", [128, 2 * 512], F32)
            fpool = sbA("fpool", [128, 8 * 512], F32)
            bpool = sbA("bpool", [128, 8 * 512], BF16)
            cT = sbA("cT_sb", [128, 16], F32)
            e16 = sbA("e16", [128, 16], F32)
            scb = sbA("scb", [128, 16], BF16)
            modsb = sbA("modsb", [128, 32], F32)
            sstmp = sbA("sstmp", [128, 4], F32)
            lntmp = sbA("lntmp", [128, 4], F32)

            def FP(i, T=512):
                return fpool[:, i * 512:i * 512 + T]

            def BP(i, T=512):
                return bpool[:, i * 512:i * 512 + T]

            bgate = fpool[0:1, 4 * 512:6 * 512]
            gate_row = fpool[0:1, 6 * 512:8 * 512]

            with ExitStack() as es1:
                tp = [es1.enter_context(nc.psum_tensor("tp%d" % i, [128, 1024], BF16)) for i in range(4)]
                mm = [es1.enter_context(nc.psum_tensor("mm%d" % i, [128, 512], F32)) for i in range(4)]
                S = Sched("a1")

                def dma(q, out, in_, key, reads=(), writes=()):
                    S.add(q, lambda e: e.dma_start(out=out, in_=in_), reads=reads, writes=writes, dma=key)

                def mmul(out, lhsT, rhs, start=True, stop=True):
                    S.add("pe", lambda e: e.matmul(out, lhsT, rhs, start=start, stop=stop),
                          reads=[lhsT, rhs] + ([] if start else [out]), writes=[out])

                def act(out, in_, func, scale=1.0, bias=None, accum=None):
                    rd = [in_]
                    kw = {}
                    if not isinstance(scale, (int, float)):
                        rd.append(scale)
                    if bias is not None:
                        kw["bias"] = bias
                        if not isinstance(bias, (int, float)):
                            rd.append(bias)
                    wr = [out]
                    if accum is not None:
                        kw["accum_out"] = accum
                        wr.append(accum)
                    S.add("act", lambda e: e.activation(out, in_, func, scale=scale, **kw), reads=rd, writes=wr)

                def tt(eng, out, in0, in1, op):
                    S.add(eng, lambda e: e.tensor_tensor(out, in0, in1, op), reads=[in0, in1], writes=[out])

                def ts(eng, out, in0, s1, s2, op0, op1=None):
                    rd = [in0] + [s for s in (s1, s2) if s is not None and not isinstance(s, (int, float))]
                    if op1 is None and eng == "pool":
                        S.add(eng, lambda e: e.tensor_scalar(out, in0, s1, 0.0, op0, ALU.add), reads=rd, writes=[out])
                    elif op1 is None:
                        S.add(eng, lambda e: e.tensor_scalar(out, in0, s1, None, op0), reads=rd, writes=[out])
                    else:
                        S.add(eng, lambda e: e.tensor_scalar(out, in0, s1, s2, op0, op1), reads=rd, writes=[out])

                def stt(out, in0, sc, in1, op0, op1):
                    rd = [in0, in1] + ([] if isinstance(sc, (int, float)) else [sc])
                    S.add("dve", lambda e: e.scalar_tensor_tensor(out, in0, sc, in1, op0, op1), reads=rd, writes=[out])

                def cp(eng, out, in_):
                    S.add(eng, lambda e: e.tensor_copy(out, in_), reads=[in_], writes=[out])

                def memset(eng, ap, v):
                    S.add(eng, lambda e: e.memset(ap, v), writes=[ap])

                dma("sp", vecs[:, :], vecs_d, ("c", 0), writes=[vecs[:, :]])
                dma("sp", cT[:, :], cT_d, ("c", 1), writes=[cT[:, :]])
                dma("sp", bgate, bgate_d, ("c", 2), writes=[bgate])
                dma("pool", ident[:, :], ident_d, ("c", 3), writes=[ident[:, :]])
                memset("pool", ones_b[:, :], 1.0)
                memset("pool", ones_f[:, :], 1.0)
                memset("pool", sel0[:, :], 0.0)
                memset("pool", sel1[:, :], 0.0)
                memset("pool", sel0[0:64, :], 1.0)
                memset("pool", sel1[64:128, :], 1.0)
                memset("pool", eps_t[:, :], EPS)
                wm_views = [V[:, 0:8192], V[:, 8192:16384], KN[:, 0:8192]]
                for s in range(3):
                    dma("pool", wm_views[s].rearrange("p (j c) -> p j c", c=1024),
                        wmod_d[:, s * 1024:(s + 1) * 1024].rearrange("(j p) c -> p j c", p=128),
                        ("wm", s), writes=[wm_views[s]])
                dma("pool", w_in_a[:, :].rearrange("p (j c) -> p j c", c=640),
                    wina_d.rearrange("(j p) c -> p j c", p=128), ("w", 0), writes=[w_in_a[:, :]])
                dma("pool", w_ukv[:, :], wukv_d, ("w", 1), writes=[w_ukv[:, :]])
                dma("pool", w_uq[:, :].rearrange("p (j c) -> p j c", c=1024),
                    wuq_d.rearrange("(j p) c -> p j c", p=128), ("w", 2), writes=[w_uq[:, :]])

                if stop == "c":
                    S.closed = True
                act(e16[:, :], cT[:, :], AF.Exp, scale=-1.0)
                ts("dve", e16[:, :], e16[:, :], 1.0, None, ALU.add)
                S.add("dve", lambda e: e.reciprocal(e16[:, :], e16[:, :]), reads=[e16[:, :]], writes=[e16[:, :]])
                tt("dve", scb[:, :], cT[:, :], e16[:, :], ALU.mult)
                psmod = mm[0]
                for s in range(2):
                    for t in range(8):
                        col = (s * 8 + t) * 2
                        for j in range(8):
                            mmul(psmod[:, col:col + 2], wm_views[s][:, j * 1024 + t * 128:j * 1024 + (t + 1) * 128],
                                 scb[:, 2 * j:2 * j + 2], start=(j == 0), stop=(j == 7))
                for r in range(2):
                    tt("dve", modsb[:, r * 16:(r + 1) * 16], psmod[:, r:32:2], vecs[:, 21:37], ALU.add)
                    stt(Gm[:, r * 8:(r + 1) * 8], modsb[:, r * 16 + 8:r * 16 + 16], 1.0, vecs[:, 0:8], ALU.add, ALU.mult)
                    cp("dve", SH[:, r * 8:(r + 1) * 8], modsb[:, r * 16:r * 16 + 8])
                for half in range(2):
                    for j in range(8):
                        mmul(mm[1 + half][0:2, :], scb[:, 2 * j:2 * j + 2],
                             wm_views[2][:, j * 1024 + half * 512:j * 1024 + (half + 1) * 512],
                             start=(j == 0), stop=(j == 7))
                    tt("dve", gate_row[0:1, half * 512:(half + 1) * 512], mm[1 + half][0:1, :],
                       bgate[0:1, half * 512:(half + 1) * 512], ALU.add)
                for half in range(2):
                    mmul(mm[1 + half][:, :], ones_f[0:1, :], gate_row[0:1, half * 512:(half + 1) * 512])
                    cp("dve", gate_bc[:, half * 512:(half + 1) * 512], mm[1 + half][:, :])

                if stop == "p0":
                    S.closed = True
                gql = [vecs[:, 8:9], vecs[:, 9:10]]
                gkvl = vecs[:, 10:11]
                qgn, qgr, qgp = vecs[:, 11:12], vecs[:, 12:13], vecs[:, 13:14]
                kgn, kgr, kgp = vecs[:, 14:15], vecs[:, 15:16], vecs[:, 16:17]
                F_LN, F_RCQ, F_RCKV, F_T2, F_ROA, F_ROB, F_R0 = 0, 1, 2, 3, 4, 5, 6
                B_SQ0, B_SQ1, B_CQ0, B_CQ1, B_SQKV, B_CKV, B_SQKR, B_SQN = range(8)

                def rms_from(ps_bank, T, nfeat, dst):
                    act(FP(F_LN, T), ps_bank[:, 0:T], AF.Ln, scale=1.0 / nfeat, bias=eps_t[:, 0:1])
                    act(dst, FP(F_LN, T), AF.Exp, scale=-0.5)

                def load_tabs(tok0, T, key):
                    dma("sp", tabs[:, :].rearrange("p (a t) -> p a t", t=512)[:, :, 0:T],
                        tabs_d[:, :, tok0:tok0 + T], ("tab", 0), writes=[tabs[:, :]])

                def chunk_geom(ci):
                    T = 256 if ci == 0 else 512
                    tok0 = 0 if ci == 0 else CTX + (ci - 1) * 512
                    g0 = 0 if ci == 0 else 2 + (ci - 1) * 4
                    return T, T // 128, tok0, g0

                def xpath_stats(ci, tiles):
                    T, nt, tok0, g0 = chunk_geom(ci)
                    for i in tiles:
                        gi = g0 + i
                        sl = gi % 2
                        xi = xin[:, sl * D:(sl + 1) * D]
                        xsi = xs[:, sl * D:(sl + 1) * D]
                        src = ctx_d[i * 128:(i + 1) * 128, :] if ci == 0 else \
                            x_d[(ci - 1) * 512 + i * 128:(ci - 1) * 512 + (i + 1) * 128, :]
                        dma("sp", xi, src, ("x", sl), writes=[xi])
                        act(xsi, xi, AF.Square, accum=sstmp[:, gi % 4:gi % 4 + 1])
                        act(lntmp[:, gi % 4:gi % 4 + 1], sstmp[:, gi % 4:gi % 4 + 1], AF.Ln, scale=1.0 / D,
                            bias=eps_t[:, 0:1])
                        act(r_all[:, gi:gi + 1], lntmp[:, gi % 4:gi % 4 + 1], AF.Exp, scale=-0.5)
                        ts("dve", xsi, xi, r_all[:, gi:gi + 1], None, ALU.mult)

                def xpath_tr(ci, tiles):
                    T, nt, tok0, g0 = chunk_geom(ci)
                    for i in tiles:
                        sl = (g0 + i) % 2
                        xsi = xs[:, sl * D:(sl + 1) * D]
                        for j in range(8):
                            o = tp[j // 2][:, (j % 2) * 512 + i * 128:(j % 2) * 512 + (i + 1) * 128]
                            inn = xsi[:, j * 128:(j + 1) * 128]
                            S.add("pe", lambda e, o=o, inn=inn: e.transpose(o, inn, ident[:, :]),
                                  reads=[inn, ident[:, :]], writes=[o])

                def xpath_evac(ci):
                    T, nt, tok0, g0 = chunk_geom(ci)
                    rsel = 1 if ci == 0 else 0
                    for j in range(8):
                        src = tp[j // 2][:, (j % 2) * 512:(j % 2) * 512 + T]
                        dst = hT[:, j * 512:j * 512 + T]
                        g_ap = Gm[:, rsel * 8 + j:rsel * 8 + j + 1]
                        s_ap = SH[:, rsel * 8 + j:rsel * 8 + j + 1]
                        act(dst, src, AF.Identity, scale=g_ap, bias=s_ap)

                def halves(ci):
                    nt = chunk_geom(ci)[1]
                    return list(range(nt // 2)), list(range(nt // 2, nt))

                h0, h1 = halves(0)
                xpath_stats(0, h0)
                xpath_tr(0, h0)
                xpath_stats(0, h1)
                xpath_tr(0, h1)
                xpath_evac(0)
                for ci in range(9):
                    if stop == "p1_%d" % ci:
                        S.closed = True
                    T, nt, tok0, g0 = chunk_geom(ci)
                    load_tabs(tok0, T, 0)
                    nxt = ci + 1 < 9 and not S.closed
                    if nxt:
                        n0, n1 = halves(ci + 1)
                        xpath_stats(ci + 1, n0)

                    def umm(bank, col0):
                        for j in range(8):
                            mmul(bank[:, 0:T], w_in_a[:, j * 640 + col0:j * 640 + col0 + 128],
                                 hT[:, j * 512:j * 512 + T], start=(j == 0), stop=(j == 7))

                    cqn = [BP(B_CQ0, T), BP(B_CQ1, T)]
                    ckvn = BP(B_CKV, T)
                    cosT = tabs[:, 0:T]
                    sinT = tabs[:, 512:512 + T]
                    sqt = [B_SQN, B_SQKV]
                    umm(mm[0], 384)
                    umm(mm[1], 512)
                    umm(mm[2], 256)
                    if nxt:
                        xpath_tr(ci + 1, n0)
                        xpath_stats(ci + 1, n1)
                    ts("pool", cosT, cosT, kgr, None, ALU.mult)
                    ts("pool", sinT, sinT, kgp, None, ALU.mult)
                    tt("dve", FP(F_ROA, T), mm[0][:, 0:T], cosT, ALU.mult)
                    tt("dve", FP(F_T2, T), mm[1][:, 0:T], sinT, ALU.mult)
                    tt("pool", FP(F_ROA, T), FP(F_ROA, T), FP(F_T2, T), ALU.add)
                    act(BP(B_SQKR, T), mm[0][:, 0:T], AF.Square)
                    act(BP(B_SQKV, T), mm[2][:, 0:T], AF.Square)
                    mmul(mm[3][:, 0:T], ones_b[:, :], BP(B_SQKV, T))
                    rms_from(mm[3], T, 128, FP(F_RCKV, T))
                    stt(BP(B_CKV, T), mm[2][:, 0:T], gkvl, FP(F_RCKV, T), ALU.mult, ALU.mult)
                    umm(mm[0], 0)
                    umm(mm[1], 128)
                    if nxt:
                        xpath_tr(ci + 1, n1)
                        xpath_evac(ci + 1)
                    act(BP(B_SQ0, T), mm[0][:, 0:T], AF.Square)
                    act(BP(B_SQ1, T), mm[1][:, 0:T], AF.Square)
                    mmul(mm[3][:, 0:T], ones_b[:, :], BP(B_SQ0, T), start=True, stop=False)
                    mmul(mm[3][:, 0:T], ones_b[:, :], BP(B_SQ1, T), start=False, stop=True)
                    rms_from(mm[3], T, 256, FP(F_RCQ, T))
                    stt(BP(B_CQ0, T), mm[0][:, 0:T], gql[0], FP(F_RCQ, T), ALU.mult, ALU.mult)
                    stt(BP(B_CQ1, T), mm[1][:, 0:T], gql[1], FP(F_RCQ, T), ALU.mult, ALU.mult)
                    lnt = [F_LN, F_ROB]
                    for hp in range(2):
                        hs = [(2 * hp + hh, hh, (mm[2], mm[3]) if hh == 0 else (mm[0], mm[1])) for hh in range(2)]
                        for h, hh, (bk, bs) in hs:
                            mmul(bk[:, 0:T], w_ukv[:, h * 128:(h + 1) * 128], ckvn)
                        for h, hh, (bk, bs) in hs:
                            act(BP(sqt[hh], T), bk[:, 0:T], AF.Square)
                        for h, hh, (bk, bs) in hs:
                            mmul(bs[:, 0:T], ones_b[:, :], BP(sqt[hh], T), start=True, stop=False)
                            mmul(bs[:, 0:T], sels[hh][:, :], BP(B_SQKR, T), start=False, stop=True)
                        for h, hh, (bk, bs) in hs:
                            act(FP(lnt[hh], T), bs[:, 0:T], AF.Ln, scale=1.0 / 192, bias=eps_t[:, 0:1])
                        for h, hh, (bk, bs) in hs:
                            act(FP(F_R0 + hh, T), FP(lnt[hh], T), AF.Exp, scale=-0.5)
                        for h, hh, (bk, bs) in hs:
                            rows = slice(hh * 64, hh * 64 + 64)
                            rk = FP(F_R0 + hh, T)
                            stt(KN[:, h * NTOK + tok0:h * NTOK + tok0 + T], bk[:, 0:T], kgn, rk, ALU.mult, ALU.mult)
                            tt("pool", KR[rows, hp * NTOK + tok0:hp * NTOK + tok0 + T],
                               fpool[rows, F_ROA * 512:F_ROA * 512 + T],
                               fpool[rows, (F_R0 + hh) * 512:(F_R0 + hh) * 512 + T], ALU.mult)
                    for i in range(nt):
                        kt = tok0 // 128 + i
                        bank = mm[i % 2]
                        mmul(bank[:, :], ckvn[:, i * 128:(i + 1) * 128], w_ukv[:, 512:1024])
                        if i % 2 == 0:
                            cp("dve", V[:, kt * 512:(kt + 1) * 512], bank[:, :])
                        else:
                            act(V[:, kt * 512:(kt + 1) * 512], bank[:, :], AF.Copy)
                    if ci == 0:
                        continue
                    q0 = (ci - 1) * 512
                    load_tabs(tok0, T, 0)
                    ts("pool", cosT, cosT, qgr, None, ALU.mult)
                    ts("pool", sinT, sinT, qgp, None, ALU.mult)
                    ro = [F_ROA, F_ROB]
                    sqr = [B_SQ0, B_SQ1]
                    for p in range(2):
                        for kk in range(2):
                            mmul(mm[0][:, :], w_uq[:, kk * 1024 + 512 + p * 128:kk * 1024 + 512 + (p + 1) * 128], cqn[kk],
                                 start=(kk == 0), stop=(kk == 1))
                        for kk in range(2):
                            mmul(mm[1][:, :], w_uq[:, kk * 1024 + 768 + p * 128:kk * 1024 + 768 + (p + 1) * 128], cqn[kk],
                                 start=(kk == 0), stop=(kk == 1))
                        tt("dve", FP(ro[p]), mm[0][:, :], cosT, ALU.mult)
                        tt("dve", FP(F_T2), mm[1][:, :], sinT, ALU.mult)
                        tt("pool", FP(ro[p]), FP(ro[p]), FP(F_T2), ALU.add)
                        act(BP(sqr[p]), mm[0][:, :], AF.Square)
                    b0 = 4 * (ci - 1)
                    pv = lambda a: a.rearrange("p (b i) -> p b i", i=128)
                    lnq = [F_LN, F_T2]
                    for hp in range(2):
                        hs = [(2 * hp + hh, hh, (mm[2], mm[3]) if hh == 0 else (mm[0], mm[1])) for hh in range(2)]
                        for h, hh, (bk, bs) in hs:
                            for kk in range(2):
                                mmul(bk[:, :], w_uq[:, kk * 1024 + h * 128:kk * 1024 + (h + 1) * 128], cqn[kk],
                                     start=(kk == 0), stop=(kk == 1))
                        for h, hh, (bk, bs) in hs:
                            act(BP(sqt[hh]), bk[:, :], AF.Square)
                        for h, hh, (bk, bs) in hs:
                            mmul(bs[:, :], ones_b[:, :], BP(sqt[hh]), start=True, stop=False)
                            mmul(bs[:, :], sels[hh][:, :], BP(sqr[hp]), start=False, stop=True)
                        for h, hh, (bk, bs) in hs:
                            act(FP(lnq[hh]), bs[:, :], AF.Ln, scale=1.0 / 192, bias=eps_t[:, 0:1])
                        for h, hh, (bk, bs) in hs:
                            act(FP(F_R0 + hh), FP(lnq[hh]), AF.Exp, scale=-0.5)
                        for h, hh, (bk, bs) in hs:
                            rows = slice(hh * 64, hh * 64 + 64)
                            rq = FP(F_R0 + hh)
                            dq = QN[:, h * L:(h + 1) * L].rearrange("p (i b) -> p b i", b=32)[:, b0:b0 + 4, :]
                            stt(dq, pv(bk[:, :]), qgn, pv(rq), ALU.mult, ALU.mult)
                            dr = QR[rows, hp * L:(hp + 1) * L].rearrange("p (i b) -> p b i", b=32)[:, b0:b0 + 4, :]
                            tt("pool", dr, pv(fpool[rows, ro[hp] * 512:(ro[hp] + 1) * 512]),
                               pv(fpool[rows, (F_R0 + hh) * 512:(F_R0 + hh + 1) * 512]), ALU.mult)

                prev = _run_block(nc, es, S, prev)
                if stop is not None and (stop in ("a1", "c", "p0") or stop.startswith("p1_")):
                    return nc

            with ExitStack() as es2:
                sT = [es2.enter_context(nc.psum_tensor("sT%d" % i, [128, 1024], F32)) for i in range(2)]
                Ob = [es2.enter_context(nc.psum_tensor("Ob%d" % i, [128, 512], F32)) for i in range(2)]
                smb = es2.enter_context(nc.psum_tensor("smb", [128, 512], F32))
                S = Sched("a2")
                SCALE = 192.0 ** -0.5
                NG = NKT // 2

                def mmul2(out, lhsT, rhs, start=True, stop=True):
                    S.add("pe", lambda e: e.matmul(out, lhsT, rhs, start=start, stop=stop),
                          reads=[lhsT, rhs] + ([] if start else [out]), writes=[out])

                stage = [bpool[:, 6 * 512:7 * 512], bpool[:, 7 * 512:8 * 512]]
                for t_ in stage:
                    S.add("pool", lambda e, t_=t_: e.memset(t_, 0.0), writes=[t_])

                def stage_copy(itn):
                    cn, hn = itn // 4, itn % 4
                    rw = slice((hn % 2) * 64, (hn % 2) * 64 + 64)
                    src = QR[rw, (hn // 2) * L + cn * 512:(hn // 2) * L + (cn + 1) * 512]
                    dst = bpool[rw, (6 + hn % 2) * 512:(7 + hn % 2) * 512]
                    S.add("pool", lambda e: e.tensor_copy(dst, src), reads=[src], writes=[dst])

                pending = None
                late = []
                it = 0
                for c in range(8):
                    for h in range(4):
                        hp, hh = h // 2, h % 2
                        ob = Ob[it % 2]
                        sset = (it % 2) * 3
                        accs = [fpool[:, (sset + k) * 512:(sset + k + 1) * 512] for k in (0, 1, 2)]
                        lnb = fpool[:, 6 * 512:7 * 512]
                        qn = QN[:, h * L + c * 512:h * L + (c + 1) * 512]
                        qr = stage[hh]

                        def qk(g, h=h, hp=hp, qn=qn, qr=qr):
                            for u in range(2):
                                j = 2 * g + u
                                o = sT[g % 2][:, u * 512:(u + 1) * 512]
                                mmul2(o, KN[:, h * NTOK + j * 128:h * NTOK + (j + 1) * 128], qn, start=True, stop=False)
                                mmul2(o, KR[:, hp * NTOK + j * 128:hp * NTOK + (j + 1) * 128], qr,
                                      start=False, stop=True)

                        if it == 0:
                            stage_copy(0)
                        qk(0)
                        qk(1)
                        if it + 1 < 32:
                            stage_copy(it + 1)
                        seen = [False, False, False]
                        for g in range(NG):
                            pt = bpool[:, (g % 3) * 1024:(g % 3 + 1) * 1024]
                            st = sT[g % 2][:, :]
                            S.add("act", lambda e, pt=pt, st=st: e.activation(pt, st, AF.Exp, scale=SCALE),
                                  reads=[st], writes=[pt])
                            for u in range(2):
                                if g % 6 == 5:
                                    ai, eng = 2, "pool"
                                else:
                                    ai, eng = u, "dve"
                                accb = accs[ai]
                                pth = pt[:, u * 512:(u + 1) * 512]
                                if not seen[ai]:
                                    seen[ai] = True
                                    S.add(eng, lambda e, accb=accb, pth=pth: e.tensor_copy(accb, pth), reads=[pth], writes=[accb])
                                else:
                                    S.add(eng, lambda e, accb=accb, pth=pth: e.tensor_tensor(accb, accb, pth, ALU.add),
                                          reads=[accb, pth], writes=[accb])
                            for u in range(2):
                                j = 2 * g + u
                                mmul2(ob[:, :], V[:, j * 512 + h * 128:j * 512 + (h + 1) * 128],
                                      pt[:, u * 512:(u + 1) * 512], start=(j == 0), stop=(j == NKT - 1))
                            if g + 2 < NG:
                                qk(g + 2)
                            if g == 3 and pending is not None:
                                late = pending()
                                pending = None
                            elif g > 3 and late:
                                late.pop(0)()

                        def fin(ob=ob, accs=accs, qn=qn, lnb=lnb):
                            S.add("dve", lambda e: e.tensor_tensor(accs[0], accs[0], accs[1], ALU.add),
                                  reads=[accs[0], accs[1]], writes=[accs[0]])
                            S.add("dve", lambda e: e.tensor_tensor(accs[1], accs[0], accs[2], ALU.add),
                                  reads=[accs[0], accs[2]], writes=[accs[1]])
                            todo = [lambda: None, lambda: mmul2(smb[:, :], ones_f[:, :], accs[1])]
                            for q4 in range(4):
                                a_, b_ = lnb[:, q4 * 128:(q4 + 1) * 128], smb[:, q4 * 128:(q4 + 1) * 128]
                                todo.append(lambda a_=a_, b_=b_: S.add("dve", lambda e: e.reciprocal(a_, b_),
                                                                       reads=[b_], writes=[a_]))
                            todo.append(lambda: S.add("dve", lambda e: e.tensor_tensor(qn, ob[:, :], lnb, ALU.mult),
                                                      reads=[ob[:, :], lnb], writes=[qn]))
                            return todo

                        pending = fin
                        it += 1
                for fn_ in pending():
                    fn_()
                prev = _run_block(nc, es, S, prev)
                if stop == "a2":
                    return nc

        with ExitStack() as esB:
            sbB = lambda n, shp, dt: esB.enter_context(nc.sbuf_tensor(n, shp, dt))
            wb = [sbB("w_in_b%d_sb" % i, [128, 8 * 512], BF16) for i in range(3)]
            w_out = sbB("w_out_sb", [128, 8 * D], BF16)
            w_pool = sbB("w_pool_sb", [128, 512], BF16)
            xin = sbB("xinB", [128, 4 * D], F32)
            xs = sbB("xsB", [128, 4 * D], BF16)
            hT = sbB("hTB", [128, 3 * 4096], BF16)
            Lh = sbB("Lh", [128, 128], BF16)
            sg = sbB("sg", [128, 8 * 512], BF16)
            U = sbB("U", [128, 4 * 528], F32)
            tA0 = sbB("tA", [128, 528], F32)
            tB0 = sbB("tB", [128, 528], F32)
            tC = sbB("tC", [128, 528], F32)
            tD = sbB("tD", [128, 528], F32)
            t8 = sbB("t8", [128, 8], F32)
            pooled = sbB("pooled", [128, 4 * 512], BF16)
            br = sbB("br", [128, 8 * 512], BF16)
            xres = sbB("xres", [128, 4 * D], F32)
            ot = sbB("ot", [128, 4 * D], F32)
            invc = sbB("invc_sb", [128, 64], F32)
            tpb = [esB.enter_context(nc.psum_tensor("tpB%d" % i, [128, 512], BF16)) for i in range(2)]
            mm = [esB.enter_context(nc.psum_tensor("mmB%d" % i, [128, 512], F32)) for i in range(3)]
            hl = esB.enter_context(nc.psum_tensor("hlB", [128, 512], F32))
            yb = [esB.enter_context(nc.psum_tensor("yB%d" % i, [128, 512], F32)) for i in range(2)]
            S = Sched("b")

            def dma(q, out, in_, key, reads=(), writes=()):
                S.add(q, lambda e: e.dma_start(out=out, in_=in_), reads=reads, writes=writes, dma=key)

            def mmul(out, lhsT, rhs, start=True, stop=True):
                S.add("pe", lambda e: e.matmul(out, lhsT, rhs, start=start, stop=stop),
                      reads=[lhsT, rhs] + ([] if start else [out]), writes=[out])

            def tt(eng, out, in0, in1, op):
                S.add(eng, lambda e: e.tensor_tensor(out, in0, in1, op), reads=[in0, in1], writes=[out])

            def stt(out, in0, sc, in1, op0, op1):
                rd = [in0, in1] + ([] if isinstance(sc, (int, float)) else [sc])
                S.add("dve", lambda e: e.scalar_tensor_tensor(out, in0, sc, in1, op0, op1), reads=rd, writes=[out])

            for wi in (1, 0, 2):
                dma("pool", wb[wi][:, :].rearrange("p (j c) -> p j c", c=512),
                    winb_d[:, wi * 512:(wi + 1) * 512].rearrange("(j p) c -> p j c", p=128), ("wb", wi),
                    writes=[wb[wi][:, :]])
            dma("pool", w_pool[:, :].rearrange("p (g d) -> p g d", d=128),
                wpool_d.rearrange("g c d -> c g d"), ("w", 1), writes=[w_pool[:, :]])
            dma("pool", w_out[:, :].rearrange("p (j c) -> p j c", c=D),
                wout_d.rearrange("(j p) c -> p j c", p=128), ("w", 2), writes=[w_out[:, :]])
            dma("sp", invc[:, :], invc_d, ("c", 0), writes=[invc[:, :]])
            for g in range(4):
                S.add("pool", lambda e, g=g: e.memset(U[:, g * 528:g * 528 + 8], 0.0), writes=[U[:, g * 528:g * 528 + 8]])
            S.add("pool", lambda e: e.memset(Lh[:, :], 0.0), writes=[Lh[:, :]])

            pscale = [vecs[:, 17 + g:18 + g] for g in range(4)]
            mmrot = [0]

            def nextmm():
                b = mm[mmrot[0] % 3]
                mmrot[0] += 1
                return b

            def stageA(c):
                hc = hT[:, (c % 3) * 4096:(c % 3 + 1) * 4096]
                for i in range(4):
                    gi = c * 4 + i
                    sl = gi % 4
                    xi = xin[:, sl * D:(sl + 1) * D]
                    dma("sp", xi, x_d[gi * 128:(gi + 1) * 128, :], ("x", sl), writes=[xi])
                    xsi = xs[:, i * D:(i + 1) * D]
                    rr = r_all[:, 2 + gi:3 + gi]
                    S.add("dve", lambda e, xsi=xsi, xi=xi, rr=rr: e.tensor_scalar(xsi, xi, rr, None, ALU.mult),
                          reads=[xi, rr], writes=[xsi])
                for j in range(8):
                    reg = j % 2
                    for i in range(4):
                        o = tpb[reg][:, i * 128:(i + 1) * 128]
                        inn = xs[:, i * D + j * 128:i * D + (j + 1) * 128]
                        S.add("pe", lambda e, o=o, inn=inn: e.transpose(o, inn, ident[:, :]),
                              reads=[inn, ident[:, :]], writes=[o])
                    src = tpb[reg][:, :]
                    dst = hc[:, j * 512:(j + 1) * 512]
                    g_ap = Gm[:, j:j + 1]
                    s_ap = SH[:, j:j + 1]
                    S.add("act", lambda e, dst=dst, src=src, g_ap=g_ap, s_ap=s_ap:
                          e.activation(dst, src, AF.Identity, scale=g_ap, bias=s_ap),
                          reads=[src, g_ap, s_ap], writes=[dst])

            WIN = [2, 4, 8, 16]

            def stageB_a(c):
                hc = hT[:, (c % 3) * 4096:(c % 3 + 1) * 4096]
                hn = hT[:, ((c + 1) % 3) * 4096:((c + 1) % 3 + 1) * 4096]
                if c < 7:
                    a = Lh[:, :].rearrange("p (j t) -> p j t", t=16)[:, :, 8:16]
                    b = hn.rearrange("p (j t) -> p j t", t=512)[:, :, 0:8]
                    S.add("pool", lambda e, a=a, b=b: e.tensor_copy(a, b), reads=[b], writes=[a])
                for g in range(4):
                    col0 = g * 128
                    bank = nextmm()
                    for j in range(8):
                        mmul(bank[:, :], wb[1][:, j * 512 + col0:j * 512 + col0 + 128], hc[:, j * 512:(j + 1) * 512],
                             start=(j == 0), stop=(j == 7))
                    for j in range(8):
                        mmul(hl[:, g * 16:g * 16 + 16], wb[1][:, j * 512 + col0:j * 512 + col0 + 128],
                             Lh[:, j * 16:(j + 1) * 16], start=(j == 0), stop=(j == 7))
                    ug = U[:, g * 528 + 8:g * 528 + 520]
                    S.add("act", lambda e, ug=ug, bank=bank: e.activation(ug, bank[:, :], AF.Copy),
                          reads=[bank[:, :]], writes=[ug])
                    if c > 0:
                        a, b = U[:, g * 528:g * 528 + 8], hl[:, g * 16:g * 16 + 8]
                        S.add("dve", lambda e, a=a, b=b: e.tensor_copy(a, b), reads=[b], writes=[a])
                    if c < 7:
                        a, b = U[:, g * 528 + 520:g * 528 + 528], hl[:, g * 16 + 8:g * 16 + 16]
                        S.add("dve", lambda e, a=a, b=b: e.tensor_copy(a, b), reads=[b], writes=[a])
                    else:
                        a = U[:, g * 528 + 520:g * 528 + 528]
                        S.add("pool", lambda e, a=a: e.memset(a, 0.0), writes=[a])
            def stageB(c):
                hc = hT[:, (c % 3) * 4096:(c % 3 + 1) * 4096]
                for g in range(4):
                    w = WIN[g]
                    u0 = g * 528
                    eng = "dve" if g % 2 == 0 else "pool"
                    tA, tB = (tA0, tB0) if g % 2 == 0 else (tC, tD)
                    tt(eng, tA[:, 0:527], U[:, u0:u0 + 527], U[:, u0 + 1:u0 + 528], ALU.add)
                    src = tA
                    if w >= 4:
                        tt(eng, tB[:, 0:525], tA[:, 0:525], tA[:, 2:527], ALU.add)
                        src = tB
                    if w >= 8:
                        tt(eng, tA[:, 0:521], tB[:, 0:521], tB[:, 4:525], ALU.add)
                        src = tA
                    if w >= 16:
                        tt(eng, tB[:, 0:513], tA[:, 0:513], tA[:, 8:521], ALU.add)
                        src = tB
                    s0 = 8 - w // 2
                    stt(pooled[:, g * 512:(g + 1) * 512], src[:, s0:s0 + 512], 1.0 / w, U[:, u0 + 8:u0 + 520],
                        ALU.mult, ALU.subtract)
                    if c == 0:
                        tt("dve", t8[:, :], src[:, s0:s0 + 8], invc[:, g * 8:(g + 1) * 8], ALU.mult)
                        tt("dve", pooled[:, g * 512:g * 512 + 8], t8[:, :], U[:, u0 + 8:u0 + 16], ALU.subtract)
                    if c == 7:
                        tt("dve", t8[:, :], src[:, s0 + 504:s0 + 512], invc[:, 32 + g * 8:32 + (g + 1) * 8], ALU.mult)
                        tt("dve", pooled[:, g * 512 + 504:g * 512 + 512], t8[:, :], U[:, u0 + 512:u0 + 520], ALU.subtract)
                for k in range(8):
                    wk = wb[0] if k < 4 else wb[2]
                    col0 = (k % 4) * 128
                    bank = nextmm()
                    for j in range(8):
                        mmul(bank[:, :], wk[:, j * 512 + col0:j * 512 + col0 + 128], hc[:, j * 512:(j + 1) * 512],
                             start=(j == 0), stop=(j == 7))
                    dst = sg[:, k * 512:(k + 1) * 512]
                    S.add("act", lambda e, dst=dst, bank=bank: e.activation(dst, bank[:, :], AF.Silu),
                          reads=[bank[:, :]], writes=[dst])
                for g in range(4):
                    bank = nextmm()
                    mmul(bank[:, :], w_pool[:, g * 128:(g + 1) * 128], pooled[:, g * 512:(g + 1) * 512])
                    stt(br[:, (4 + g) * 512:(5 + g) * 512], bank[:, :], pscale[g], sg[:, (4 + g) * 512:(5 + g) * 512],
                        ALU.mult, ALU.mult)
                for h in range(4):
                    tt("pool", br[:, h * 512:(h + 1) * 512], sg[:, h * 512:(h + 1) * 512],
                       QN[:, h * L + c * 512:h * L + (c + 1) * 512], ALU.mult)
                if c < 7:
                    a = Lh[:, :].rearrange("p (j t) -> p j t", t=16)[:, :, 0:8]
                    b = hc.rearrange("p (j t) -> p j t", t=512)[:, :, 504:512]
                    S.add("pool", lambda e, a=a, b=b: e.tensor_copy(a, b), reads=[b], writes=[a])
            def stageB2(c):
                for i in range(4):
                    gi = c * 4 + i
                    xr = xres[:, i * D:(i + 1) * D]
                    dma("sp", xr, x_d[gi * 128:(gi + 1) * 128, :], ("xr", i), writes=[xr])
                for i in range(4):
                    gi = c * 4 + i
                    sl = i
                    xr = xres[:, sl * D:(sl + 1) * D]
                    oo = ot[:, sl * D:(sl + 1) * D]
                    for half in range(2):
                        for kk in range(8):
                            mmul(yb[half][:, :], br[:, kk * 512 + i * 128:kk * 512 + (i + 1) * 128],
                                 w_out[:, kk * D + half * 512:kk * D + (half + 1) * 512], start=(kk == 0), stop=(kk == 7))
                        tt("dve", oo[:, half * 512:(half + 1) * 512], yb[half][:, :], xr[:, half * 512:(half + 1) * 512],
                           ALU.add)
                    dma("sp", out_d[gi * 128:(gi + 1) * 128, :], oo, ("o", sl), reads=[oo])

            stageA(0)
            stageA(1)
            for c in range(8):
                stageB_a(c)
                if c + 2 < 8:
                    stageA(c + 2)
                stageB(c)
                if c == 0:
                    for kk in range(8):
                        wv = w_out[:, kk * D:(kk + 1) * D]
                        tt("dve" if kk % 2 == 0 else "pool", wv, wv, gate_bc[:, :], ALU.mult)
                stageB2(c)
            prev = _run_block(nc, es, S, prev, final_waits=True)
    return nc


def _perm64():
    d = np.arange(64)
    return np.where((d % 32) < 16, d + 16, d - 16)


def _host_consts():
    perm = _perm64()
    sign = np.where((np.arange(64) % 32) < 16, -1.0, 1.0).astype(np.float32)
    t = np.arange(L)
    row = (t // 64).astype(np.float32)
    col = (t % 64).astype(np.float32)
    inv = (np.float32(10000.0) ** (-np.arange(16, dtype=np.float32) / np.float32(16))).astype(np.float32)
    ang_r = row[:, None] * inv[None, :]
    ang_c = col[:, None] * inv[None, :]
    ang = np.concatenate([ang_r, ang_r, ang_c, ang_c], axis=-1).astype(np.float32)
    cos = np.cos(ang).astype(np.float32)
    sin = np.sin(ang).astype(np.float32)
    tabs = np.zeros((128, 2, NTOK), np.float32)
    tabs[:, 0, :CTX] = 1.0
    tabs[0:64, 0, CTX:] = cos.T
    tabs[64:128, 0, CTX:] = cos.T
    tabs[0:64, 1, CTX:] = (sin * sign[None, :]).T
    tabs[64:128, 1, CTX:] = (sin * sign[None, :]).T
    invc = np.zeros((128, 64), np.float32)
    for g, w in enumerate((2, 4, 8, 16)):
        for k in range(8):
            tt_ = k
            lo = max(tt_ - w // 2, 0)
            hi = min(tt_ - w // 2 + w, L)
            invc[:, g * 8 + k] = 1.0 / (hi - lo)
            tt_ = L - 8 + k
            lo = max(tt_ - w // 2, 0)
            hi = min(tt_ - w // 2 + w, L)
            invc[:, 32 + g * 8 + k] = 1.0 / (hi - lo)
    ident = np.eye(128, dtype=np.float32)
    return tabs, invc, ident


_NC = None


def kernel(x, c, ctx, c_ctx, w_mod, b_mod, norm_g, w_in, q_lora_g, w_uq, kv_lora_g, w_ukv,
           q_norm_g, k_norm_g, w_pool, pool_scale, w_out):
    global _NC
    f = lambda a: np.ascontiguousarray(np.asarray(a, dtype=np.float32))
    x, c, ctx, c_ctx = f(x), f(c), f(ctx), f(c_ctx)
    w_mod, b_mod, norm_g, w_in = f(w_mod)[0], f(b_mod)[0], f(norm_g)[0], f(w_in)[0]
    q_lora_g, w_uq, kv_lora_g, w_ukv = f(q_lora_g)[0], f(w_uq)[0], f(kv_lora_g)[0], f(w_ukv)[0]
    q_norm_g, k_norm_g, w_pool, pool_scale, w_out = f(q_norm_g)[0], f(k_norm_g)[0], f(w_pool)[0], f(pool_scale)[0], f(w_out)[0]
    perm = _perm64()
    tabs, invc, ident = _host_consts()
    kr = w_in[:, 384:448]
    krp = kr[:, perm]
    w_in_a = f(np.concatenate([w_in[:, 0:384], kr, kr, krp, krp], axis=1))
    w_in_b = f(w_in[:, 448:1984])
    nope = [w_uq[:, h * 192:h * 192 + 128] for h in range(4)]
    rope = [w_uq[:, h * 192 + 128:h * 192 + 192] for h in range(4)]
    ropep = [r[:, perm] for r in rope]
    w_uq_x = f(np.concatenate(nope + rope + ropep, axis=1))
    w_ukv_x = f(np.concatenate([w_ukv[:, h * 256:h * 256 + 128] for h in range(4)] +
                               [w_ukv[:, h * 256 + 128:h * 256 + 256] for h in range(4)], axis=1))
    vecs = np.zeros((128, NV), np.float32)
    vecs[:, 0:8] = norm_g.reshape(8, 128).T
    vecs[:, 8:10] = q_lora_g.reshape(2, 128).T
    vecs[:, 10] = kv_lora_g
    for base, g in ((11, q_norm_g), (14, k_norm_g)):
        vecs[:, base] = g[0:128]
        gr = g[128:192]
        vecs[:, base + 1] = np.concatenate([gr, gr])
        vecs[:, base + 2] = np.concatenate([gr[perm], gr[perm]])
    vecs[:, 17:21] = pool_scale.reshape(4, 128).T
    vecs[:, 21:45] = b_mod.reshape(24, 128).T
    bgate = f(b_mod[2048:3072].reshape(1, D))
    if _NC is None:
        _NC = build_nc(stop=os.environ.get("KSTOP"))
    in_maps = []
    for b in range(8):
        cvec = np.stack([c[b], c_ctx], axis=0)
        cT = f(cvec.reshape(2, 8, 128).transpose(2, 1, 0).reshape(128, 16))
        in_maps.append({
            "x": x[b], "ctx": ctx[b], "cT": cT, "vecs": vecs, "bgate": bgate, "w_mod": w_mod,
            "w_in_a": w_in_a, "w_in_b": w_in_b, "w_uq_x": w_uq_x, "w_ukv_x": w_ukv_x,
            "w_pool": w_pool, "w_out": w_out, "tabs": tabs, "ident": ident, "invc": invc,
        })
    res = run_bass_kernel_spmd(_NC, in_maps, core_ids=list(range(8)))
    return np.stack([np.asarray(r["out"], dtype=np.float32) for r in res.results], axis=0)
```

```python
import os
import numpy as np
from contextlib import ExitStack
import concourse.bass as bass
import concourse.mybir as mybir
from concourse.bass_utils import run_bass_kernel_spmd

F32 = mybir.dt.float32
BF16 = mybir.dt.bfloat16
AF = mybir.ActivationFunctionType
ALU = mybir.AluOpType

D = 1024
L = 4096
CTX = 256
NTOK = L + CTX
NKT = NTOK // 128
EPS = 1e-6
NV = 45
SAME_ENGINE_SYNC = {"act": True, "dve": True, "pool": True, "pe": False, "sp": True}
_DSZ = {F32: 4, BF16: 2}


class _Op:
    __slots__ = ("eng", "fn", "deps", "dma", "sig")

    def __init__(self, eng, fn, deps, dma):
        self.eng, self.fn, self.deps, self.dma, self.sig = eng, fn, deps, dma, None


class Sched:
    def __init__(self, name):
        self.name = name
        self.ops = []
        self.acc = {}
        self.final = {}
        self.closed = False

    @staticmethod
    def box(ap):
        t = ap.tensor
        pstep = 1
        for s in list(t.shape)[1:]:
            pstep *= int(s)
        off = int(ap.offset)
        dims = list(ap.ap)
        p0 = off // pstep
        f0 = off % pstep
        ext = 0
        for st, cn in dims[1:]:
            ext += abs(int(st)) * (int(cn) - 1)
        p1 = p0 + int(dims[0][1])
        if type(t).__name__.startswith("PSum"):
            return (t.name, (p0 // 32) * 32, ((p1 + 31) // 32) * 32, 0, pstep)
        return (t.name, p0, p1, f0, f0 + ext + 1)

    def add(self, eng, fn, reads=(), writes=(), dma=None):
        if self.closed:
            return -1
        idx = len(self.ops)
        mx = os.environ.get("KMAXOPS")
        if mx is not None and self.name == "a1":
            if idx >= int(mx):
                self.closed = True
                return -1
            import traceback
            self.dbg = getattr(self, "dbg", [])
            self.dbg.append((eng, [f.lineno for f in traceback.extract_stack(limit=5)[:-1]]))
        deps = set()
        sk = dma if dma is not None else eng
        for ap in reads:
            nm, p0, p1, f0, f1 = self.box(ap)
            lst = self.acc.setdefault(nm, [])
            found = False
            is_psum = type(ap.tensor).__name__.startswith("PSum")
            for e in lst:
                if e[0] < p1 and p0 < e[1] and e[2] < f1 and f0 < e[3]:
                    if e[5] or (is_psum and e[6] != sk):
                        deps.add(e[4])
                    elif (not found) and e[6] == sk and e[0] == p0 and e[1] == p1 and e[2] == f0 and e[3] == f1:
                        e[4] = idx
                        found = True
            if not found:
                lst.append([p0, p1, f0, f1, idx, False, sk])
        for ap in writes:
            nm, p0, p1, f0, f1 = self.box(ap)
            lst = self.acc.setdefault(nm, [])
            keep = []
            for e in lst:
                if e[0] < p1 and p0 < e[1] and e[2] < f1 and f0 < e[3]:
                    deps.add(e[4])
                    if e[0] >= p0 and e[1] <= p1 and e[2] >= f0 and e[3] <= f1:
                        continue
                keep.append(e)
            keep.append([p0, p1, f0, f1, idx, True, sk])
            self.acc[nm] = keep
        deps.discard(idx)
        self.ops.append(_Op(eng, fn, sorted(deps), dma))
        return idx

    @staticmethod
    def _needs(p, o):
        if p.dma is not None or o.dma is not None:
            return True
        if p.eng != o.eng:
            return True
        return SAME_ENGINE_SYNC.get(o.eng, True)

    def finalize(self):
        n = len(self.ops)
        needed = [False] * n
        for o in self.ops:
            for d in o.deps:
                if self._needs(self.ops[d], o):
                    needed[d] = True
        cnt = {}
        last = {}
        for i, o in enumerate(self.ops):
            last[o.eng] = i
        for i, o in enumerate(self.ops):
            if o.dma is not None:
                cnt[o.dma] = cnt.get(o.dma, 0) + 16
                o.sig = (o.dma, cnt[o.dma])
            elif needed[i] or last[o.eng] == i:
                cnt[o.eng] = cnt.get(o.eng, 0) + 1
                o.sig = (o.eng, cnt[o.eng])
        self.final = dict(cnt)
        return list(cnt.keys())

    def emit_engine(self, engname, e, sems, prologue=()):
        waited = {}
        for sem, val in prologue:
            e.wait_ge(sem, val)
        for o in self.ops:
            if o.eng != engname:
                continue
            w = {}
            for d in o.deps:
                p = self.ops[d]
                if p.sig is None or not self._needs(p, o):
                    continue
                k, v = p.sig
                if waited.get(k, 0) >= v:
                    continue
                if w.get(k, 0) < v:
                    w[k] = v
            for k, v in w.items():
                e.wait_ge(sems[k], v)
                waited[k] = v
            ins = o.fn(e)
            if o.sig is not None:
                ins.then_inc(sems[o.sig[0]], 16 if o.dma is not None else 1)


def _run_block(nc, es, sched, prev, final_waits=False):
    keys = sched.finalize()
    sems = {}
    for k in keys:
        nm = "s_%s_%s" % (sched.name, "_".join(str(x) for x in (k if isinstance(k, tuple) else (k,))))
        sems[k] = es.enter_context(nc.semaphore(nm))
    with nc.Block() as block:
        def mk(engname, with_final):
            def body(e):
                sched.emit_engine(engname, e, sems, prologue=prev)
                if with_final:
                    for k, v in sched.final.items():
                        e.wait_ge(sems[k], v)
            return body

        block.tensor(mk("pe", False))
        block.scalar(mk("act", False))
        block.vector(mk("dve", False))
        block.gpsimd(mk("pool", False))
        block.sync(mk("sp", final_waits))
    return [(sems[k], v) for k, v in sched.final.items()]


def build_nc(stop=None):
    nc = bass.Bass("TRN2", target_bir_lowering=False)
    dram = lambda n, shp: nc.dram_tensor(n, shp, F32, kind="ExternalInput").ap()
    x_d = dram("x", [L, D])
    ctx_d = dram("ctx", [CTX, D])
    cT_d = dram("cT", [128, 16])
    vecs_d = dram("vecs", [128, NV])
    bgate_d = dram("bgate", [1, D])
    wmod_d = dram("w_mod", [D, 3 * D])
    wina_d = dram("w_in_a", [D, 640])
    winb_d = dram("w_in_b", [D, 1536])
    wuq_d = dram("w_uq_x", [256, 1024])
    wukv_d = dram("w_ukv_x", [128, 1024])
    wpool_d = dram("w_pool", [4, 128, 128])
    wout_d = dram("w_out", [D, D])
    tabs_d = dram("tabs", [128, 2, NTOK])
    ident_d = dram("ident", [128, 128])
    invc_d = dram("invc", [128, 64])
    out_d = nc.dram_tensor("out", [L, D], F32, kind="ExternalOutput").ap()

    with ExitStack() as es:
        sb = lambda n, shp, dt: es.enter_context(nc.sbuf_tensor(n, shp, dt))
        QN = sb("QN", [128, 4 * L], BF16)
        gate_bc = sb("gate_bc", [128, D], F32)
        r_all = sb("r_all", [128, 36], F32)
        vecs = sb("vecs_sb", [128, NV], F32)
        Gm = sb("Gm", [128, 16], F32)
        SH = sb("SH", [128, 16], F32)
        ident = sb("ident_sb", [128, 128], BF16)
        ones_b = sb("ones_b", [128, 128], BF16)
        sel0 = sb("sel0", [128, 128], BF16)
        sel1 = sb("sel1", [128, 128], BF16)
        ones_f = sb("ones_f", [128, 128], F32)
        eps_t = sb("eps_t", [128, 1], F32)
        sels = [sel0, sel1]

        prev = []
        with ExitStack() as esA:
            sbA = lambda n, shp, dt: esA.enter_context(nc.sbuf_tensor(n, shp, dt))
            QR = sbA("QR", [128, 2 * L], BF16)
            KN = sbA("KN", [128, 4 * NTOK], BF16)
            KR = sbA("KR", [128, 2 * NTOK], BF16)
            V = sbA("V", [128, NKT * 512], BF16)
            w_in_a = sbA("w_in_a_sb", [128, 8 * 640], BF16)
            w_uq = sbA("w_uq_sb", [128, 2 * 1024], BF16)
            w_ukv = sbA("w_ukv_sb", [128, 1024], BF16)
            xin = sbA("xin", [128, 2 * D], F32)
            xs = sbA("xs", [128, 2 * D], BF16)
            hT = sbA("hT", [128, 8 * 512], BF16)
            tabs = sbA("tabs_sb", [128, 2 * 512], F32)
            fpool = sbA("fpool", [128, 8 * 512], F32)
            bpool = sbA("bpool", [128, 8 * 512], BF16)
            cT = sbA("cT_sb", [128, 16], F32)
            e16 = sbA("e16", [128, 16], F32)
            scb = sbA("scb", [128, 16], BF16)
            modsb = sbA("modsb", [128, 32], F32)
            sstmp = sbA("sstmp", [128, 4], F32)
            lntmp = sbA("lntmp", [128, 4], F32)

            def FP(i, T=512):
                return fpool[:, i * 512:i * 512 + T]

            def BP(i, T=512):
                return bpool[:, i * 512:i * 512 + T]

            bgate = fpool[0:1, 4 * 512:6 * 512]
            gate_row = fpool[0:1, 6 * 512:8 * 512]

            with ExitStack() as es1:
                tp = [es1.enter_context(nc.psum_tensor("tp%d" % i, [128, 1024], BF16)) for i in range(4)]
                mm = [es1.enter_context(nc.psum_tensor("mm%d" % i, [128, 512], F32)) for i in range(4)]
                S = Sched("a1")

                def dma(q, out, in_, key, reads=(), writes=()):
                    S.add(q, lambda e: e.dma_start(out=out, in_=in_), reads=reads, writes=writes, dma=key)

                def mmul(out, lhsT, rhs, start=True, stop=True):
                    S.add("pe", lambda e: e.matmul(out, lhsT, rhs, start=start, stop=stop),
                          reads=[lhsT, rhs] + ([] if start else [out]), writes=[out])

                def act(out, in_, func, scale=1.0, bias=None, accum=None):
                    rd = [in_]
                    kw = {}
                    if not isinstance(scale, (int, float)):
                        rd.append(scale)
                    if bias is not None:
                        kw["bias"] = bias
                        if not isinstance(bias, (int, float)):
                            rd.append(bias)
                    wr = [out]
                    if accum is not None:
                        kw["accum_out"] = accum
                        wr.append(accum)
                    S.add("act", lambda e: e.activation(out, in_, func, scale=scale, **kw), reads=rd, writes=wr)

                def tt(eng, out, in0, in1, op):
                    S.add(eng, lambda e: e.tensor_tensor(out, in0, in1, op), reads=[in0, in1], writes=[out])

                def ts(eng, out, in0, s1, s2, op0, op1=None):
                    rd = [in0] + [s for s in (s1, s2) if s is not None and not isinstance(s, (int, float))]
                    if op1 is None and eng == "pool":
                        S.add(eng, lambda e: e.tensor_scalar(out, in0, s1, 0.0, op0, ALU.add), reads=rd, writes=[out])
                    elif op1 is None:
                        S.add(eng, lambda e: e.tensor_scalar(out, in0, s1, None, op0), reads=rd, writes=[out])
                    else:
                        S.add(eng, lambda e: e.tensor_scalar(out, in0, s1, s2, op0, op1), reads=rd, writes=[out])

                def stt(out, in0, sc, in1, op0, op1):
                    rd = [in0, in1] + ([] if isinstance(sc, (int, float)) else [sc])
                    S.add("dve", lambda e: e.scalar_tensor_tensor(out, in0, sc, in1, op0, op1), reads=rd, writes=[out])

                def cp(eng, out, in_):
                    S.add(eng, lambda e: e.tensor_copy(out, in_), reads=[in_], writes=[out])

                def memset(eng, ap, v):
                    S.add(eng, lambda e: e.memset(ap, v), writes=[ap])

                dma("sp", vecs[:, :], vecs_d, ("c", 0), writes=[vecs[:, :]])
                dma("sp", cT[:, :], cT_d, ("c", 1), writes=[cT[:, :]])
                dma("sp", bgate, bgate_d, ("c", 2), writes=[bgate])
                dma("pool", ident[:, :], ident_d, ("c", 3), writes=[ident[:, :]])
                memset("pool", ones_b[:, :], 1.0)
                memset("pool", ones_f[:, :], 1.0)
                memset("pool", sel0[:, :], 0.0)
                memset("pool", sel1[:, :], 0.0)
                memset("pool", sel0[0:64, :], 1.0)
                memset("pool", sel1[64:128, :], 1.0)
                memset("pool", eps_t[:, :], EPS)
                wm_views = [V[:, 0:8192], V[:, 8192:16384], KN[:, 0:8192]]
                for s in range(3):
                    dma("pool", wm_views[s].rearrange("p (j c) -> p j c", c=1024),
                        wmod_d[:, s * 1024:(s + 1) * 1024].rearrange("(j p) c -> p j c", p=128),
                        ("wm", s), writes=[wm_views[s]])
                dma("pool", w_in_a[:, :].rearrange("p (j c) -> p j c", c=640),
                    wina_d.rearrange("(j p) c -> p j c", p=128), ("w", 0), writes=[w_in_a[:, :]])
                dma("pool", w_ukv[:, :], wukv_d, ("w", 1), writes=[w_ukv[:, :]])
                dma("pool", w_uq[:, :].rearrange("p (j c) -> p j c", c=1024),
                    wuq_d.rearrange("(j p) c -> p j c", p=128), ("w", 2), writes=[w_uq[:, :]])

                if stop == "c":
                    S.closed = True
                act(e16[:, :], cT[:, :], AF.Exp, scale=-1.0)
                ts("dve", e16[:, :], e16[:, :], 1.0, None, ALU.add)
                S.add("dve", lambda e: e.reciprocal(e16[:, :], e16[:, :]), reads=[e16[:, :]], writes=[e16[:, :]])
                tt("dve", scb[:, :], cT[:, :], e16[:, :], ALU.mult)
                psmod = mm[0]
                for s in range(2):
                    for t in range(8):
                        col = (s * 8 + t) * 2
                        for j in range(8):
                            mmul(psmod[:, col:col + 2], wm_views[s][:, j * 1024 + t * 128:j * 1024 + (t + 1) * 128],
                                 scb[:, 2 * j:2 * j + 2], start=(j == 0), stop=(j == 7))
                for r in range(2):
                    tt("dve", modsb[:, r * 16:(r + 1) * 16], psmod[:, r:32:2], vecs[:, 21:37], ALU.add)
                    stt(Gm[:, r * 8:(r + 1) * 8], modsb[:, r * 16 + 8:r * 16 + 16], 1.0, vecs[:, 0:8], ALU.add, ALU.mult)
                    cp("dve", SH[:, r * 8:(r + 1) * 8], modsb[:, r * 16:r * 16 + 8])
                for half in range(2):
                    for j in range(8):
                        mmul(mm[1 + half][0:2, :], scb[:, 2 * j:2 * j + 2],
                             wm_views[2][:, j * 1024 + half * 512:j * 1024 + (half + 1) * 512],
                             start=(j == 0), stop=(j == 7))
                    tt("dve", gate_row[0:1, half * 512:(half + 1) * 512], mm[1 + half][0:1, :],
                       bgate[0:1, half * 512:(half + 1) * 512], ALU.add)
                for half in range(2):
                    mmul(mm[1 + half][:, :], ones_f[0:1, :], gate_row[0:1, half * 512:(half + 1) * 512])
                    cp("dve", gate_bc[:, half * 512:(half + 1) * 512], mm[1 + half][:, :])

                if stop == "p0":
                    S.closed = True
                gql = [vecs[:, 8:9], vecs[:, 9:10]]
                gkvl = vecs[:, 10:11]
                qgn, qgr, qgp = vecs[:, 11:12], vecs[:, 12:13], vecs[:, 13:14]
                kgn, kgr, kgp = vecs[:, 14:15], vecs[:, 15:16], vecs[:, 16:17]
                F_LN, F_RCQ, F_RCKV, F_T2, F_ROA, F_ROB, F_R0 = 0, 1, 2, 3, 4, 5, 6
                B_SQ0, B_SQ1, B_CQ0, B_CQ1, B_SQKV, B_CKV, B_SQKR, B_SQN = range(8)

                def rms_from(ps_bank, T, nfeat, dst):
                    act(FP(F_LN, T), ps_bank[:, 0:T], AF.Ln, scale=1.0 / nfeat, bias=eps_t[:, 0:1])
                    act(dst, FP(F_LN, T), AF.Exp, scale=-0.5)

                def load_tabs(tok0, T, key):
                    dma("sp", tabs[:, :].rearrange("p (a t) -> p a t", t=512)[:, :, 0:T],
                        tabs_d[:, :, tok0:tok0 + T], ("tab", 0), writes=[tabs[:, :]])

                def chunk_geom(ci):
                    T = 256 if ci == 0 else 512
                    tok0 = 0 if ci == 0 else CTX + (ci - 1) * 512
                    g0 = 0 if ci == 0 else 2 + (ci - 1) * 4
                    return T, T // 128, tok0, g0

                def xpath_stats(ci, tiles):
                    T, nt, tok0, g0 = chunk_geom(ci)
                    for i in tiles:
                        gi = g0 + i
                        sl = gi % 2
                        xi = xin[:, sl * D:(sl + 1) * D]
                        xsi = xs[:, sl * D:(sl + 1) * D]
                        src = ctx_d[i * 128:(i + 1) * 128, :] if ci == 0 else \
                            x_d[(ci - 1) * 512 + i * 128:(ci - 1) * 512 + (i + 1) * 128, :]
                        dma("sp", xi, src, ("x", sl), writes=[xi])
                        act(xsi, xi, AF.Square, accum=sstmp[:, gi % 4:gi % 4 + 1])
                        act(lntmp[:, gi % 4:gi % 4 + 1], sstmp[:, gi % 4:gi % 4 + 1], AF.Ln, scale=1.0 / D,
                            bias=eps_t[:, 0:1])
                        act(r_all[:, gi:gi + 1], lntmp[:, gi % 4:gi % 4 + 1], AF.Exp, scale=-0.5)
                        ts("dve", xsi, xi, r_all[:, gi:gi + 1], None, ALU.mult)

                def xpath_tr(ci, tiles):
                    T, nt, tok0, g0 = chunk_geom(ci)
                    for i in tiles:
                        sl = (g0 + i) % 2
                        xsi = xs[:, sl * D:(sl + 1) * D]
                        for j in range(8):
                            o = tp[j // 2][:, (j % 2) * 512 + i * 128:(j % 2) * 512 + (i + 1) * 128]
                            inn = xsi[:, j * 128:(j + 1) * 128]
                            S.add("pe", lambda e, o=o, inn=inn: e.transpose(o, inn, ident[:, :]),
                                  reads=[inn, ident[:, :]], writes=[o])

                def xpath_evac(ci):
                    T, nt, tok0, g0 = chunk_geom(ci)
                    rsel = 1 if ci == 0 else 0
                    for j in range(8):
                        src = tp[j // 2][:, (j % 2) * 512:(j % 2) * 512 + T]
                        dst = hT[:, j * 512:j * 512 + T]
                        g_ap = Gm[:, rsel * 8 + j:rsel * 8 + j + 1]
                        s_ap = SH[:, rsel * 8 + j:rsel * 8 + j + 1]
                        act(dst, src, AF.Identity, scale=g_ap, bias=s_ap)

                def halves(ci):
                    nt = chunk_geom(ci)[1]
                    return list(range(nt // 2)), list(range(nt // 2, nt))

                h0, h1 = halves(0)
                xpath_stats(0, h0)
                xpath_tr(0, h0)
                xpath_stats(0, h1)
                xpath_tr(0, h1)
                xpath_evac(0)
                for ci in range(9):
                    if stop == "p1_%d" % ci:
                        S.closed = True
                    T, nt, tok0, g0 = chunk_geom(ci)
                    load_tabs(tok0, T, 0)
                    nxt = ci + 1 < 9 and not S.closed
                    if nxt:
                        n0, n1 = halves(ci + 1)
                        xpath_stats(ci + 1, n0)

                    def umm(bank, col0):
                        for j in range(8):
                            mmul(bank[:, 0:T], w_in_a[:, j * 640 + col0:j * 640 + col0 + 128],
                                 hT[:, j * 512:j * 512 + T], start=(j == 0), stop=(j == 7))

                    cqn = [BP(B_CQ0, T), BP(B_CQ1, T)]
                    ckvn = BP(B_CKV, T)
                    cosT = tabs[:, 0:T]
                    sinT = tabs[:, 512:512 + T]
                    sqt = [B_SQN, B_SQKV]
                    umm(mm[0], 384)
                    umm(mm[1], 512)
                    umm(mm[2], 256)
                    if nxt:
                        xpath_tr(ci + 1, n0)
                        xpath_stats(ci + 1, n1)
                    ts("pool", cosT, cosT, kgr, None, ALU.mult)
                    ts("pool", sinT, sinT, kgp, None, ALU.mult)
                    tt("dve", FP(F_ROA, T), mm[0][:, 0:T], cosT, ALU.mult)
                    tt("dve", FP(F_T2, T), mm[1][:, 0:T], sinT, ALU.mult)
                    tt("pool", FP(F_ROA, T), FP(F_ROA, T), FP(F_T2, T), ALU.add)
                    act(BP(B_SQKR, T), mm[0][:, 0:T], AF.Square)
                    act(BP(B_SQKV, T), mm[2][:, 0:T], AF.Square)
                    mmul(mm[3][:, 0:T], ones_b[:, :], BP(B_SQKV, T))
                    rms_from(mm[3], T, 128, FP(F_RCKV, T))
                    stt(BP(B_CKV, T), mm[2][:, 0:T], gkvl, FP(F_RCKV, T), ALU.mult, ALU.mult)
                    umm(mm[0], 0)
                    umm(mm[1], 128)
                    if nxt:
                        xpath_tr(ci + 1, n1)
                        xpath_evac(ci + 1)
                    act(BP(B_SQ0, T), mm[0][:, 0:T], AF.Square)
                    act(BP(B_SQ1, T), mm[1][:, 0:T], AF.Square)
                    mmul(mm[3][:, 0:T], ones_b[:, :], BP(B_SQ0, T), start=True, stop=False)
                    mmul(mm[3][:, 0:T], ones_b[:, :], BP(B_SQ1, T), start=False, stop=True)
                    rms_from(mm[3], T, 256, FP(F_RCQ, T))
                    stt(BP(B_CQ0, T), mm[0][:, 0:T], gql[0], FP(F_RCQ, T), ALU.mult, ALU.mult)
                    stt(BP(B_CQ1, T), mm[1][:, 0:T], gql[1], FP(F_RCQ, T), ALU.mult, ALU.mult)
                    lnt = [F_LN, F_ROB]
                    for hp in range(2):
                        hs = [(2 * hp + hh, hh, (mm[2], mm[3]) if hh == 0 else (mm[0], mm[1])) for hh in range(2)]
                        for h, hh, (bk, bs) in hs:
                            mmul(bk[:, 0:T], w_ukv[:, h * 128:(h + 1) * 128], ckvn)
                        for h, hh, (bk, bs) in hs:
                            act(BP(sqt[hh], T), bk[:, 0:T], AF.Square)
                        for h, hh, (bk, bs) in hs:
                            mmul(bs[:, 0:T], ones_b[:, :], BP(sqt[hh], T), start=True, stop=False)
                            mmul(bs[:, 0:T], sels[hh][:, :], BP(B_SQKR, T), start=False, stop=True)
                        for h, hh, (bk, bs) in hs:
                            act(FP(lnt[hh], T), bs[:, 0:T], AF.Ln, scale=1.0 / 192, bias=eps_t[:, 0:1])
                        for h, hh, (bk, bs) in hs:
                            act(FP(F_R0 + hh, T), FP(lnt[hh], T), AF.Exp, scale=-0.5)
                        for h, hh, (bk, bs) in hs:
                            rows = slice(hh * 64, hh * 64 + 64)
                            rk = FP(F_R0 + hh, T)
                            stt(KN[:, h * NTOK + tok0:h * NTOK + tok0 + T], bk[:, 0:T], kgn, rk, ALU.mult, ALU.mult)
                            tt("pool", KR[rows, hp * NTOK + tok0:hp * NTOK + tok0 + T],
                               fpool[rows, F_ROA * 512:F_ROA * 512 + T],
                               fpool[rows, (F_R0 + hh) * 512:(F_R0 + hh) * 512 + T], ALU.mult)
                    for i in range(nt):
                        kt = tok0 // 128 + i
                        bank = mm[i % 2]
                        mmul(bank[:, :], ckvn[:, i * 128:(i + 1) * 128], w_ukv[:, 512:1024])
                        cp("dve", V[:, kt * 512:(kt + 1) * 512], bank[:, :])
                    if ci == 0:
                        continue
                    q0 = (ci - 1) * 512
                    load_tabs(tok0, T, 0)
                    ts("pool", cosT, cosT, qgr, None, ALU.mult)
                    ts("pool", sinT, sinT, qgp, None, ALU.mult)
                    ro = [F_ROA, F_ROB]
                    sqr = [B_SQ0, B_SQ1]
                    for p in range(2):
                        for kk in range(2):
                            mmul(mm[0][:, :], w_uq[:, kk * 1024 + 512 + p * 128:kk * 1024 + 512 + (p + 1) * 128], cqn[kk],
                                 start=(kk == 0), stop=(kk == 1))
                        for kk in range(2):
                            mmul(mm[1][:, :], w_uq[:, kk * 1024 + 768 + p * 128:kk * 1024 + 768 + (p + 1) * 128], cqn[kk],
                                 start=(kk == 0), stop=(kk == 1))
                        tt("dve", FP(ro[p]), mm[0][:, :], cosT, ALU.mult)
                        tt("dve", FP(F_T2), mm[1][:, :], sinT, ALU.mult)
                        tt("pool", FP(ro[p]), FP(ro[p]), FP(F_T2), ALU.add)
                        act(BP(sqr[p]), mm[0][:, :], AF.Square)
                    b0 = 4 * (ci - 1)
                    pv = lambda a: a.rearrange("p (b i) -> p b i", i=128)
                    lnq = [F_LN, F_T2]
                    for hp in range(2):
                        hs = [(2 * hp + hh, hh, (mm[2], mm[3]) if hh == 0 else (mm[0], mm[1])) for hh in range(2)]
                        for h, hh, (bk, bs) in hs:
                            for kk in range(2):
                                mmul(bk[:, :], w_uq[:, kk * 1024 + h * 128:kk * 1024 + (h + 1) * 128], cqn[kk],
                                     start=(kk == 0), stop=(kk == 1))
                        for h, hh, (bk, bs) in hs:
                            act(BP(sqt[hh]), bk[:, :], AF.Square)
                        for h, hh, (bk, bs) in hs:
                            mmul(bs[:, :], ones_b[:, :], BP(sqt[hh]), start=True, stop=False)
                            mmul(bs[:, :], sels[hh][:, :], BP(sqr[hp]), start=False, stop=True)
                        for h, hh, (bk, bs) in hs:
                            act(FP(lnq[hh]), bs[:, :], AF.Ln, scale=1.0 / 192, bias=eps_t[:, 0:1])
                        for h, hh, (bk, bs) in hs:
                            act(FP(F_R0 + hh), FP(lnq[hh]), AF.Exp, scale=-0.5)
                        for h, hh, (bk, bs) in hs:
                            rows = slice(hh * 64, hh * 64 + 64)
                            rq = FP(F_R0 + hh)
                            dq = QN[:, h * L:(h + 1) * L].rearrange("p (i b) -> p b i", b=32)[:, b0:b0 + 4, :]
                            stt(dq, pv(bk[:, :]), qgn, pv(rq), ALU.mult, ALU.mult)
                            dr = QR[rows, hp * L:(hp + 1) * L].rearrange("p (i b) -> p b i", b=32)[:, b0:b0 + 4, :]
                            tt("pool", dr, pv(fpool[rows, ro[hp] * 512:(ro[hp] + 1) * 512]),
                               pv(fpool[rows, (F_R0 + hh) * 512:(F_R0 + hh + 1) * 512]), ALU.mult)

                prev = _run_block(nc, es, S, prev)
                if stop is not None and (stop in ("a1", "c", "p0") or stop.startswith("p1_")):
                    return nc

            with ExitStack() as es2:
                sT = [es2.enter_context(nc.psum_tensor("sT%d" % i, [128, 1024], F32)) for i in range(2)]
                Ob = [es2.enter_context(nc.psum_tensor("Ob%d" % i, [128, 512], F32)) for i in range(2)]
                smb = es2.enter_context(nc.psum_tensor("smb", [128, 512], F32))
                S = Sched("a2")
                SCALE = 192.0 ** -0.5
                NG = NKT // 2

                def mmul2(out, lhsT, rhs, start=True, stop=True):
                    S.add("pe", lambda e: e.matmul(out, lhsT, rhs, start=start, stop=stop),
                          reads=[lhsT, rhs] + ([] if start else [out]), writes=[out])

                stage = [bpool[:, 6 * 512:7 * 512], bpool[:, 7 * 512:8 * 512]]
                for t_ in stage:
                    S.add("pool", lambda e, t_=t_: e.memset(t_, 0.0), writes=[t_])

                def stage_copy(itn):
                    cn, hn = itn // 4, itn % 4
                    rw = slice((hn % 2) * 64, (hn % 2) * 64 + 64)
                    src = QR[rw, (hn // 2) * L + cn * 512:(hn // 2) * L + (cn + 1) * 512]
                    dst = bpool[rw, (6 + hn % 2) * 512:(7 + hn % 2) * 512]
                    S.add("pool", lambda e: e.tensor_copy(dst, src), reads=[src], writes=[dst])

                pending = None
                late = []
                it = 0
                for c in range(8):
                    for h in range(4):
                        hp, hh = h // 2, h % 2
                        ob = Ob[it % 2]
                        sset = (it % 2) * 3
                        accs = [fpool[:, (sset + k) * 512:(sset + k + 1) * 512] for k in (0, 1, 2)]
                        lnb = fpool[:, 6 * 512:7 * 512]
                        qn = QN[:, h * L + c * 512:h * L + (c + 1) * 512]
                        qr = stage[hh]

                        def qk(g, h=h, hp=hp, qn=qn, qr=qr):
                            for u in range(2):
                                j = 2 * g + u
                                o = sT[g % 2][:, u * 512:(u + 1) * 512]
                                mmul2(o, KN[:, h * NTOK + j * 128:h * NTOK + (j + 1) * 128], qn, start=True, stop=False)
                                mmul2(o, KR[:, hp * NTOK + j * 128:hp * NTOK + (j + 1) * 128], qr,
                                      start=False, stop=True)

                        if it == 0:
                            stage_copy(0)
                        qk(0)
                        qk(1)
                        if it + 1 < 32:
                            stage_copy(it + 1)
                        seen = [False, False, False]
                        for g in range(NG):
                            pt = bpool[:, (g % 3) * 1024:(g % 3 + 1) * 1024]
                            st = sT[g % 2][:, :]
                            S.add("act", lambda e, pt=pt, st=st: e.activation(pt, st, AF.Exp, scale=SCALE),
                                  reads=[st], writes=[pt])
                            for u in range(2):
                                if g % 4 == 3:
                                    ai, eng = 2, "pool"
                                else:
                                    ai, eng = u, "dve"
                                accb = accs[ai]
                                pth = pt[:, u * 512:(u + 1) * 512]
                                if not seen[ai]:
                                    seen[ai] = True
                                    S.add(eng, lambda e, accb=accb, pth=pth: e.tensor_copy(accb, pth), reads=[pth], writes=[accb])
                                else:
                                    S.add(eng, lambda e, accb=accb, pth=pth: e.tensor_tensor(accb, accb, pth, ALU.add),
                                          reads=[accb, pth], writes=[accb])
                            for u in range(2):
                                j = 2 * g + u
                                mmul2(ob[:, :], V[:, j * 512 + h * 128:j * 512 + (h + 1) * 128],
                                      pt[:, u * 512:(u + 1) * 512], start=(j == 0), stop=(j == NKT - 1))
                            if g + 2 < NG:
                                qk(g + 2)
                            if g == 3 and pending is not None:
                                late = pending()
                                pending = None
                            elif g > 3 and late:
                                late.pop(0)()

                        def fin(ob=ob, accs=accs, qn=qn, lnb=lnb):
                            S.add("dve", lambda e: e.tensor_tensor(accs[0], accs[0], accs[1], ALU.add),
                                  reads=[accs[0], accs[1]], writes=[accs[0]])
                            S.add("dve", lambda e: e.tensor_tensor(accs[1], accs[0], accs[2], ALU.add),
                                  reads=[accs[0], accs[2]], writes=[accs[1]])
                            todo = [lambda: None, lambda: mmul2(smb[:, :], ones_f[:, :], accs[1])]
                            for q4 in range(4):
                                a_, b_ = lnb[:, q4 * 128:(q4 + 1) * 128], smb[:, q4 * 128:(q4 + 1) * 128]
                                todo.append(lambda a_=a_, b_=b_: S.add("dve", lambda e: e.reciprocal(a_, b_),
                                                                       reads=[b_], writes=[a_]))
                            todo.append(lambda: S.add("dve", lambda e: e.tensor_tensor(qn, ob[:, :], lnb, ALU.mult),
                                                      reads=[ob[:, :], lnb], writes=[qn]))
                            return todo

                        pending = fin
                        it += 1
                for fn_ in pending():
                    fn_()
                prev = _run_block(nc, es, S, prev)
                if stop == "a2":
                    return nc

        with ExitStack() as esB:
            sbB = lambda n, shp, dt: esB.enter_context(nc.sbuf_tensor(n, shp, dt))
            wb = [sbB("w_in_b%d_sb" % i, [128, 8 * 512], BF16) for i in range(3)]
            w_out = sbB("w_out_sb", [128, 8 * D], BF16)
            w_pool = sbB("w_pool_sb", [128, 512], BF16)
            xin = sbB("xinB", [128, 4 * D], F32)
            xs = sbB("xsB", [128, 4 * D], BF16)
            hT = sbB("hTB", [128, 3 * 4096], BF16)
            Lh = sbB("Lh", [128, 128], BF16)
            sg = sbB("sg", [128, 8 * 512], BF16)
            U = sbB("U", [128, 4 * 528], F32)
            tA0 = sbB("tA", [128, 528], F32)
            tB0 = sbB("tB", [128, 528], F32)
            tC = sbB("tC", [128, 528], F32)
            tD = sbB("tD", [128, 528], F32)
            t8 = sbB("t8", [128, 8], F32)
            pooled = sbB("pooled", [128, 4 * 512], BF16)
            br = sbB("br", [128, 8 * 512], BF16)
            xres = sbB("xres", [128, 4 * D], F32)
            ot = sbB("ot", [128, 4 * D], F32)
            invc = sbB("invc_sb", [128, 64], F32)
            tpb = [esB.enter_context(nc.psum_tensor("tpB%d" % i, [128, 512], BF16)) for i in range(2)]
            mm = [esB.enter_context(nc.psum_tensor("mmB%d" % i, [128, 512], F32)) for i in range(3)]
            hl = esB.enter_context(nc.psum_tensor("hlB", [128, 512], F32))
            yb = [esB.enter_context(nc.psum_tensor("yB%d" % i, [128, 512], F32)) for i in range(2)]
            S = Sched("b")

            def dma(q, out, in_, key, reads=(), writes=()):
                S.add(q, lambda e: e.dma_start(out=out, in_=in_), reads=reads, writes=writes, dma=key)

            def mmul(out, lhsT, rhs, start=True, stop=True):
                S.add("pe", lambda e: e.matmul(out, lhsT, rhs, start=start, stop=stop),
                      reads=[lhsT, rhs] + ([] if start else [out]), writes=[out])

            def tt(eng, out, in0, in1, op):
                S.add(eng, lambda e: e.tensor_tensor(out, in0, in1, op), reads=[in0, in1], writes=[out])

            def stt(out, in0, sc, in1, op0, op1):
                rd = [in0, in1] + ([] if isinstance(sc, (int, float)) else [sc])
                S.add("dve", lambda e: e.scalar_tensor_tensor(out, in0, sc, in1, op0, op1), reads=rd, writes=[out])

            for wi in (1, 0, 2):
                dma("pool", wb[wi][:, :].rearrange("p (j c) -> p j c", c=512),
                    winb_d[:, wi * 512:(wi + 1) * 512].rearrange("(j p) c -> p j c", p=128), ("wb", wi),
                    writes=[wb[wi][:, :]])
            dma("pool", w_pool[:, :].rearrange("p (g d) -> p g d", d=128),
                wpool_d.rearrange("g c d -> c g d"), ("w", 1), writes=[w_pool[:, :]])
            dma("pool", w_out[:, :].rearrange("p (j c) -> p j c", c=D),
                wout_d.rearrange("(j p) c -> p j c", p=128), ("w", 2), writes=[w_out[:, :]])
            dma("sp", invc[:, :], invc_d, ("c", 0), writes=[invc[:, :]])
            for g in range(4):
                S.add("pool", lambda e, g=g: e.memset(U[:, g * 528:g * 528 + 8], 0.0), writes=[U[:, g * 528:g * 528 + 8]])
            S.add("pool", lambda e: e.memset(Lh[:, :], 0.0), writes=[Lh[:, :]])

            pscale = [vecs[:, 17 + g:18 + g] for g in range(4)]
            mmrot = [0]

            def nextmm():
                b = mm[mmrot[0] % 3]
                mmrot[0] += 1
                return b

            def stageA(c):
                hc = hT[:, (c % 3) * 4096:(c % 3 + 1) * 4096]
                for i in range(4):
                    gi = c * 4 + i
                    sl = gi % 4
                    xi = xin[:, sl * D:(sl + 1) * D]
                    dma("sp", xi, x_d[gi * 128:(gi + 1) * 128, :], ("x", sl), writes=[xi])
                    xsi = xs[:, i * D:(i + 1) * D]
                    rr = r_all[:, 2 + gi:3 + gi]
                    S.add("dve", lambda e, xsi=xsi, xi=xi, rr=rr: e.tensor_scalar(xsi, xi, rr, None, ALU.mult),
                          reads=[xi, rr], writes=[xsi])
                for j in range(8):
                    reg = j % 2
                    for i in range(4):
                        o = tpb[reg][:, i * 128:(i + 1) * 128]
                        inn = xs[:, i * D + j * 128:i * D + (j + 1) * 128]
                        S.add("pe", lambda e, o=o, inn=inn: e.transpose(o, inn, ident[:, :]),
                              reads=[inn, ident[:, :]], writes=[o])
                    src = tpb[reg][:, :]
                    dst = hc[:, j * 512:(j + 1) * 512]
                    g_ap = Gm[:, j:j + 1]
                    s_ap = SH[:, j:j + 1]
                    S.add("act", lambda e, dst=dst, src=src, g_ap=g_ap, s_ap=s_ap:
                          e.activation(dst, src, AF.Identity, scale=g_ap, bias=s_ap),
                          reads=[src, g_ap, s_ap], writes=[dst])

            WIN = [2, 4, 8, 16]

            def stageB_a(c):
                hc = hT[:, (c % 3) * 4096:(c % 3 + 1) * 4096]
                hn = hT[:, ((c + 1) % 3) * 4096:((c + 1) % 3 + 1) * 4096]
                if c < 7:
                    a = Lh[:, :].rearrange("p (j t) -> p j t", t=16)[:, :, 8:16]
                    b = hn.rearrange("p (j t) -> p j t", t=512)[:, :, 0:8]
                    S.add("pool", lambda e, a=a, b=b: e.tensor_copy(a, b), reads=[b], writes=[a])
                for g in range(4):
                    col0 = g * 128
                    bank = nextmm()
                    for j in range(8):
                        mmul(bank[:, :], wb[1][:, j * 512 + col0:j * 512 + col0 + 128], hc[:, j * 512:(j + 1) * 512],
                             start=(j == 0), stop=(j == 7))
                    for j in range(8):
                        mmul(hl[:, g * 16:g * 16 + 16], wb[1][:, j * 512 + col0:j * 512 + col0 + 128],
                             Lh[:, j * 16:(j + 1) * 16], start=(j == 0), stop=(j == 7))
                    ug = U[:, g * 528 + 8:g * 528 + 520]
                    S.add("act", lambda e, ug=ug, bank=bank: e.activation(ug, bank[:, :], AF.Copy),
                          reads=[bank[:, :]], writes=[ug])
                    if c > 0:
                        a, b = U[:, g * 528:g * 528 + 8], hl[:, g * 16:g * 16 + 8]
                        S.add("dve", lambda e, a=a, b=b: e.tensor_copy(a, b), reads=[b], writes=[a])
                    if c < 7:
                        a, b = U[:, g * 528 + 520:g * 528 + 528], hl[:, g * 16 + 8:g * 16 + 16]
                        S.add("dve", lambda e, a=a, b=b: e.tensor_copy(a, b), reads=[b], writes=[a])
                    else:
                        a = U[:, g * 528 + 520:g * 528 + 528]
                        S.add("pool", lambda e, a=a: e.memset(a, 0.0), writes=[a])
            def stageB(c):
                hc = hT[:, (c % 3) * 4096:(c % 3 + 1) * 4096]
                for g in range(4):
                    w = WIN[g]
                    u0 = g * 528
                    eng = "dve" if g % 2 == 0 else "pool"
                    tA, tB = (tA0, tB0) if g % 2 == 0 else (tC, tD)
                    tt(eng, tA[:, 0:527], U[:, u0:u0 + 527], U[:, u0 + 1:u0 + 528], ALU.add)
                    src = tA
                    if w >= 4:
                        tt(eng, tB[:, 0:525], tA[:, 0:525], tA[:, 2:527], ALU.add)
                        src = tB
                    if w >= 8:
                        tt(eng, tA[:, 0:521], tB[:, 0:521], tB[:, 4:525], ALU.add)
                        src = tA
                    if w >= 16:
                        tt(eng, tB[:, 0:513], tA[:, 0:513], tA[:, 8:521], ALU.add)
                        src = tB
                    s0 = 8 - w // 2
                    stt(pooled[:, g * 512:(g + 1) * 512], src[:, s0:s0 + 512], 1.0 / w, U[:, u0 + 8:u0 + 520],
                        ALU.mult, ALU.subtract)
                    if c == 0:
                        tt("dve", t8[:, :], src[:, s0:s0 + 8], invc[:, g * 8:(g + 1) * 8], ALU.mult)
                        tt("dve", pooled[:, g * 512:g * 512 + 8], t8[:, :], U[:, u0 + 8:u0 + 16], ALU.subtract)
                    if c == 7:
                        tt("dve", t8[:, :], src[:, s0 + 504:s0 + 512], invc[:, 32 + g * 8:32 + (g + 1) * 8], ALU.mult)
                        tt("dve", pooled[:, g * 512 + 504:g * 512 + 512], t8[:, :], U[:, u0 + 512:u0 + 520], ALU.subtract)
                for k in range(8):
                    wk = wb[0] if k < 4 else wb[2]
                    col0 = (k % 4) * 128
                    bank = nextmm()
                    for j in range(8):
                        mmul(bank[:, :], wk[:, j * 512 + col0:j * 512 + col0 + 128], hc[:, j * 512:(j + 1) * 512],
                             start=(j == 0), stop=(j == 7))
                    dst = sg[:, k * 512:(k + 1) * 512]
                    S.add("act", lambda e, dst=dst, bank=bank: e.activation(dst, bank[:, :], AF.Silu),
                          reads=[bank[:, :]], writes=[dst])
                for g in range(4):
                    bank = nextmm()
                    mmul(bank[:, :], w_pool[:, g * 128:(g + 1) * 128], pooled[:, g * 512:(g + 1) * 512])
                    stt(br[:, (4 + g) * 512:(5 + g) * 512], bank[:, :], pscale[g], sg[:, (4 + g) * 512:(5 + g) * 512],
                        ALU.mult, ALU.mult)
                for h in range(4):
                    tt("pool", br[:, h * 512:(h + 1) * 512], sg[:, h * 512:(h + 1) * 512],
                       QN[:, h * L + c * 512:h * L + (c + 1) * 512], ALU.mult)
                if c < 7:
                    a = Lh[:, :].rearrange("p (j t) -> p j t", t=16)[:, :, 0:8]
                    b = hc.rearrange("p (j t) -> p j t", t=512)[:, :, 504:512]
                    S.add("pool", lambda e, a=a, b=b: e.tensor_copy(a, b), reads=[b], writes=[a])
            def stageB2(c):
                for i in range(4):
                    gi = c * 4 + i
                    xr = xres[:, i * D:(i + 1) * D]
                    dma("sp", xr, x_d[gi * 128:(gi + 1) * 128, :], ("xr", i), writes=[xr])
                for i in range(4):
                    gi = c * 4 + i
                    sl = i
                    xr = xres[:, sl * D:(sl + 1) * D]
                    oo = ot[:, sl * D:(sl + 1) * D]
                    for half in range(2):
                        for kk in range(8):
                            mmul(yb[half][:, :], br[:, kk * 512 + i * 128:kk * 512 + (i + 1) * 128],
                                 w_out[:, kk * D + half * 512:kk * D + (half + 1) * 512], start=(kk == 0), stop=(kk == 7))
                        tt("dve", oo[:, half * 512:(half + 1) * 512], yb[half][:, :], xr[:, half * 512:(half + 1) * 512],
                           ALU.add)
                    dma("sp", out_d[gi * 128:(gi + 1) * 128, :], oo, ("o", sl), reads=[oo])

            stageA(0)
            stageA(1)
            for c in range(8):
                stageB_a(c)
                if c + 2 < 8:
                    stageA(c + 2)
                stageB(c)
                if c == 0:
                    for kk in range(8):
                        wv = w_out[:, kk * D:(kk + 1) * D]
                        tt("dve" if kk % 2 == 0 else "pool", wv, wv, gate_bc[:, :], ALU.mult)
                stageB2(c)
            prev = _run_block(nc, es, S, prev, final_waits=True)
    return nc


def _perm64():
    d = np.arange(64)
    return np.where((d % 32) < 16, d + 16, d - 16)


def _host_consts():
    perm = _perm64()
    sign = np.where((np.arange(64) % 32) < 16, -1.0, 1.0).astype(np.float32)
    t = np.arange(L)
    row = (t // 64).astype(np.float32)
    col = (t % 64).astype(np.float32)
    inv = (np.float32(10000.0) ** (-np.arange(16, dtype=np.float32) / np.float32(16))).astype(np.float32)
    ang_r = row[:, None] * inv[None, :]
    ang_c = col[:, None] * inv[None, :]
    ang = np.concatenate([ang_r, ang_r, ang_c, ang_c], axis=-1).astype(np.float32)
    cos = np.cos(ang).astype(np.float32)
    sin = np.sin(ang).astype(np.float32)
    tabs = np.zeros((128, 2, NTOK), np.float32)
    tabs[:, 0, :CTX] = 1.0
    tabs[0:64, 0, CTX:] = cos.T
    tabs[64:128, 0, CTX:] = cos.T
    tabs[0:64, 1, CTX:] = (sin * sign[None, :]).T
    tabs[64:128, 1, CTX:] = (sin * sign[None, :]).T
    invc = np.zeros((128, 64), np.float32)
    for g, w in enumerate((2, 4, 8, 16)):
        for k in range(8):
            tt_ = k
            lo = max(tt_ - w // 2, 0)
            hi = min(tt_ - w // 2 + w, L)
            invc[:, g * 8 + k] = 1.0 / (hi - lo)
            tt_ = L - 8 + k
            lo = max(tt_ - w // 2, 0)
            hi = min(tt_ - w // 2 + w, L)
            invc[:, 32 + g * 8 + k] = 1.0 / (hi - lo)
    ident = np.eye(128, dtype=np.float32)
    return tabs, invc, ident


_NC = None


def kernel(x, c, ctx, c_ctx, w_mod, b_mod, norm_g, w_in, q_lora_g, w_uq, kv_lora_g, w_ukv,
           q_norm_g, k_norm_g, w_pool, pool_scale, w_out):
    global _NC
    f = lambda a: np.ascontiguousarray(np.asarray(a, dtype=np.float32))
    x, c, ctx, c_ctx = f(x), f(c), f(ctx), f(c_ctx)
    w_mod, b_mod, norm_g, w_in = f(w_mod)[0], f(b_mod)[0], f(norm_g)[0], f(w_in)[0]
    q_lora_g, w_uq, kv_lora_g, w_ukv = f(q_lora_g)[0], f(w_uq)[0], f(kv_lora_g)[0], f(w_ukv)[0]
    q_norm_g, k_norm_g, w_pool, pool_scale, w_out = f(q_norm_g)[0], f(k_norm_g)[0], f(w_pool)[0], f(pool_scale)[0], f(w_out)[0]
    perm = _perm64()
    tabs, invc, ident = _host_consts()
    kr = w_in[:, 384:448]
    krp = kr[:, perm]
    w_in_a = f(np.concatenate([w_in[:, 0:384], kr, kr, krp, krp], axis=1))
    w_in_b = f(w_in[:, 448:1984])
    nope = [w_uq[:, h * 192:h * 192 + 128] for h in range(4)]
    rope = [w_uq[:, h * 192 + 128:h * 192 + 192] for h in range(4)]
    ropep = [r[:, perm] for r in rope]
    w_uq_x = f(np.concatenate(nope + rope + ropep, axis=1))
    w_ukv_x = f(np.concatenate([w_ukv[:, h * 256:h * 256 + 128] for h in range(4)] +
                               [w_ukv[:, h * 256 + 128:h * 256 + 256] for h in range(4)], axis=1))
    vecs = np.zeros((128, NV), np.float32)
    vecs[:, 0:8] = norm_g.reshape(8, 128).T
    vecs[:, 8:10] = q_lora_g.reshape(2, 128).T
    vecs[:, 10] = kv_lora_g
    for base, g in ((11, q_norm_g), (14, k_norm_g)):
        vecs[:, base] = g[0:128]
        gr = g[128:192]
        vecs[:, base + 1] = np.concatenate([gr, gr])
        vecs[:, base + 2] = np.concatenate([gr[perm], gr[perm]])
    vecs[:, 17:21] = pool_scale.reshape(4, 128).T
    vecs[:, 21:45] = b_mod.reshape(24, 128).T
    bgate = f(b_mod[2048:3072].reshape(1, D))
    if _NC is None:
        _NC = build_nc(stop=os.environ.get("KSTOP"))
    in_maps = []
    for b in range(8):
        cvec = np.stack([c[b], c_ctx], axis=0)
        cT = f(cvec.reshape(2, 8, 128).transpose(2, 1, 0).reshape(128, 16))
        in_maps.append({
            "x": x[b], "ctx": ctx[b], "cT": cT, "vecs": vecs, "bgate": bgate, "w_mod": w_mod,
            "w_in_a": w_in_a, "w_in_b": w_in_b, "w_uq_x": w_uq_x, "w_ukv_x": w_ukv_x,
            "w_pool": w_pool, "w_out": w_out, "tabs": tabs, "ident": ident, "invc": invc,
        })
    res = run_bass_kernel_spmd(_NC, in_maps, core_ids=list(range(8)))
    return np.stack([np.asarray(r["out"], dtype=np.float32) for r in res.results], axis=0)
```
